# Optimizing a Trainium2 kernel written in Bass

```python
import jax, jax.numpy as jnp
from jax import lax
import numpy as np

D_MODEL = 1024
BATCH = 16
SEQ = 2048
DEPTH = 1
DEC_BATCH = 16
DEC_SEQ = 32
PAST_LEN = 2048

CHUNK = 64
D_RNN = 1024
N_RNN_HEADS = 16
RNN_HEAD_DIM = D_RNN // N_RNN_HEADS
CONV_W = 4
LRU_C = 8.0
D_GMLP = 1024
N_GMLP_GROUPS = 8
GMLP_GROUP_DIM = D_GMLP // N_GMLP_GROUPS
MLP_CHUNK = 128
D_FF = ((8 * D_MODEL // 3 + 255) // 256) * 256
D_IN = 2 * D_RNN + 2 * D_GMLP + 2 * D_MODEL
EPS = 1e-6

kernel_name = "hawk_gmlp_parallel_streaming_encoder"


def rmsnorm(x, g):
    xf = x.astype(jnp.float32)
    y = xf * lax.rsqrt(jnp.mean(xf * xf, axis=-1, keepdims=True) + EPS) * g.astype(jnp.float32)
    return y.astype(x.dtype)


def layernorm(x, g, b):
    xf = x.astype(jnp.float32)
    mu = jnp.mean(xf, axis=-1, keepdims=True)
    var = jnp.mean(jnp.square(xf - mu), axis=-1, keepdims=True)
    y = (xf - mu) * lax.rsqrt(var + EPS) * g.astype(jnp.float32) + b.astype(jnp.float32)
    return y.astype(x.dtype)


def causal_conv(x, prev, w, b):
    T = x.shape[1]
    xp = jnp.concatenate([prev.astype(x.dtype), x], axis=1)
    y = b + sum(xp[:, k:k + T] * w[k] for k in range(CONV_W))
    return y, xp[:, -(CONV_W - 1):]


def _lin_combine(left, right):
    a1, b1 = left
    a2, b2 = right
    return a1 * a2, a2 * b1 + b2


def rglru(x, h0, w_a, b_a, w_x, b_x, lam):
    B, T, _ = x.shape
    xh = x.reshape(B, T, N_RNN_HEADS, RNN_HEAD_DIM)
    r = jax.nn.sigmoid(jnp.einsum('bthi,hij->bthj', xh, w_a) + b_a).reshape(B, T, D_RNN)
    i = jax.nn.sigmoid(jnp.einsum('bthi,hij->bthj', xh, w_x) + b_x).reshape(B, T, D_RNN)
    log_a = -LRU_C * r.astype(jnp.float32) * jax.nn.softplus(-lam.astype(jnp.float32))
    a = jnp.exp(log_a)
    mult = jnp.sqrt(-jnp.expm1(2.0 * log_a))
    u = mult * (i * x).astype(jnp.float32)
    u = u.at[:, 0].add(a[:, 0] * h0.astype(jnp.float32))
    _, h = lax.associative_scan(_lin_combine, (a, u), axis=1)
    return h.astype(x.dtype), h[:, -1].astype(x.dtype)


def spatial_gate(v, w_s, b_s):
    B, T, _ = v.shape
    L = min(T, MLP_CHUNK)
    N = T // L
    vh = v.reshape(B, N, L, N_GMLP_GROUPS, GMLP_GROUP_DIM)
    pos = jnp.arange(L)
    mask = (pos[None, :] // CHUNK) <= (pos[:, None] // CHUNK)
    w = jnp.where(mask[None], w_s[:, :L, :L], jnp.zeros((), w_s.dtype))
    out = jnp.einsum('gpq,bnqgd->bnpgd', w, vh) + b_s[:, :L].T[None, None, :, :, None]
    return out.reshape(B, T, D_GMLP)


def layer(x, h0, conv_prev, g_pre_mix, w_in, conv_w, conv_b, w_a, b_a, w_x, b_x, lam,
          w_br_a, ln_g, ln_b, w_s, b_s, w_br_b, w_out, g_post_mix,
          g_pre_ffn, w_ffn_in, w_ffn_out, g_post_ffn):
    xn = rmsnorm(x, g_pre_mix)
    z = xn @ w_in
    xa, ga, u, v, gates = jnp.split(
        z, [D_RNN, 2 * D_RNN, 2 * D_RNN + D_GMLP, 2 * D_RNN + 2 * D_GMLP], axis=-1)
    xc, conv_new = causal_conv(xa, conv_prev, conv_w, conv_b)
    hseq, h_last = rglru(xc, h0, w_a, b_a, w_x, b_x, lam)
    o_a = (hseq * jax.nn.gelu(ga)) @ w_br_a
    vn = layernorm(jax.nn.gelu(v), ln_g, ln_b)
    o_b = (jax.nn.gelu(u) * spatial_gate(vn, w_s, b_s)) @ w_br_b
    g_a, g_b = jnp.split(jax.nn.sigmoid(gates), 2, axis=-1)
    mix = (g_a * o_a + g_b * o_b) @ w_out
    x = x + rmsnorm(mix, g_post_mix)
    hn = rmsnorm(x, g_pre_ffn)
    gate, up = jnp.split(hn @ w_ffn_in, 2, axis=-1)
    f = (jax.nn.silu(gate) * up) @ w_ffn_out
    x = x + rmsnorm(f, g_post_ffn)
    return x, h_last, conv_new, vn


def setup_inputs(seed: int = 0) -> dict:
    key = jax.random.key(seed)
    ks = jax.random.split(key, 32)
    f32 = jnp.float32

    def nrm(k, shape, scale):
        return jax.random.normal(k, shape, f32) * scale

    def gain(k, shape):
        return 1.0 + 0.05 * jax.random.normal(k, shape, f32)

    a0 = jax.random.uniform(ks[11], (DEPTH, D_RNN), f32, 0.9, 0.999)
    sp = -jnp.log(a0) / LRU_C
    lam = -jnp.log(jnp.expm1(sp))
    return {
        "x_prompt": nrm(ks[0], (BATCH, SEQ, D_MODEL), 1.0),
        "x_sample": nrm(ks[1], (DEC_BATCH, DEC_SEQ, D_MODEL), 1.0),
        "state_rglru_h": nrm(ks[2], (DEPTH, DEC_BATCH, D_RNN), 0.5),
        "state_rglru_conv": nrm(ks[3], (DEPTH, DEC_BATCH, CONV_W - 1, D_RNN), 1.0),
        "g_pre_mix": gain(ks[4], (DEPTH, D_MODEL)),
        "w_in": nrm(ks[5], (DEPTH, D_MODEL, D_IN), D_MODEL ** -0.5),
        "conv_w": nrm(ks[6], (DEPTH, CONV_W, D_RNN), CONV_W ** -0.5),
        "conv_b": nrm(ks[7], (DEPTH, D_RNN), 0.01),
        "w_a": nrm(ks[8], (DEPTH, N_RNN_HEADS, RNN_HEAD_DIM, RNN_HEAD_DIM), RNN_HEAD_DIM ** -0.5),
        "b_a": nrm(ks[9], (DEPTH, N_RNN_HEADS, RNN_HEAD_DIM), 0.01),
        "w_x": nrm(ks[10], (DEPTH, N_RNN_HEADS, RNN_HEAD_DIM, RNN_HEAD_DIM), RNN_HEAD_DIM ** -0.5),
        "b_x": nrm(ks[12], (DEPTH, N_RNN_HEADS, RNN_HEAD_DIM), 0.01),
        "lam": lam,
        "w_br_a": nrm(ks[13], (DEPTH, D_RNN, D_MODEL), D_RNN ** -0.5),
        "ln_g": gain(ks[14], (DEPTH, D_GMLP)),
        "ln_b": nrm(ks[15], (DEPTH, D_GMLP), 0.01),
        "w_s": nrm(ks[16], (DEPTH, N_GMLP_GROUPS, MLP_CHUNK, MLP_CHUNK), MLP_CHUNK ** -0.5),
        "b_s": gain(ks[17], (DEPTH, N_GMLP_GROUPS, MLP_CHUNK)),
        "w_br_b": nrm(ks[18], (DEPTH, D_GMLP, D_MODEL), D_GMLP ** -0.5),
        "w_out": nrm(ks[19], (DEPTH, D_MODEL, D_MODEL), D_MODEL ** -0.5),
        "g_post_mix": gain(ks[20], (DEPTH, D_MODEL)),
        "g_pre_ffn": gain(ks[21], (DEPTH, D_MODEL)),
        "w_ffn_in": nrm(ks[22], (DEPTH, D_MODEL, 2 * D_FF), D_MODEL ** -0.5),
        "w_ffn_out": nrm(ks[23], (DEPTH, D_FF, D_MODEL), D_FF ** -0.5),
        "g_post_ffn": gain(ks[24], (DEPTH, D_MODEL)),
    }


def reference(x_prompt, x_sample, state_rglru_h, state_rglru_conv,
              g_pre_mix, w_in, conv_w, conv_b, w_a, b_a, w_x, b_x, lam,
              w_br_a, ln_g, ln_b, w_s, b_s, w_br_b, w_out, g_post_mix,
              g_pre_ffn, w_ffn_in, w_ffn_out, g_post_ffn):
    xp = x_prompt
    xs = x_sample
    bp = x_prompt.shape[0]
    hp_list, cp_list, hs_list, cs_list, vs_list = [], [], [], [], []
    for l in range(DEPTH):
        params = (g_pre_mix[l], w_in[l], conv_w[l], conv_b[l], w_a[l], b_a[l], w_x[l], b_x[l], lam[l],
                  w_br_a[l], ln_g[l], ln_b[l], w_s[l], b_s[l], w_br_b[l], w_out[l], g_post_mix[l],
                  g_pre_ffn[l], w_ffn_in[l], w_ffn_out[l], g_post_ffn[l])
        h0_p = jnp.zeros((bp, D_RNN), xp.dtype)
        c0_p = jnp.zeros((bp, CONV_W - 1, D_RNN), xp.dtype)
        xp, hp, cp, _ = layer(xp, h0_p, c0_p, *params)
        xs, hs, cs, vs = layer(xs, state_rglru_h[l], state_rglru_conv[l], *params)
        hp_list.append(hp)
        cp_list.append(cp)
        hs_list.append(hs)
        cs_list.append(cs)
        vs_list.append(vs)
    new_h_prompt = jnp.stack(hp_list)
    new_conv_prompt = jnp.stack(cp_list)
    new_h_sample = jnp.stack(hs_list)
    new_conv_sample = jnp.stack(cs_list)
    new_v_sample = jnp.stack(vs_list)
    return (xp, xs, new_h_prompt, new_conv_prompt, new_h_sample, new_conv_sample, new_v_sample)
```

```python
import contextlib
import numpy as np
import concourse.bass as bass
import concourse.mybir as mybir
from concourse.bass_utils import run_bass_kernel_spmd

F32 = mybir.dt.float32
BF16 = mybir.dt.bfloat16
AF = mybir.ActivationFunctionType
ALU = mybir.AluOpType

NCORES = 8
D = 1024
NCH = 8
DFF = 2816
NFF = 22
SEQ = 2048
DEC = 32
EPS = 1e-6
TT = 256
NSLOT = 8
CV_AHEAD = 24
INTERLEAVE = True
F_OFFSET = 0.37
FOUT_W = 1.0


class _Op:
    __slots__ = ("eng", "fn", "deps", "needed", "tok", "dma_sem")

    def __init__(self, eng, fn, dma_sem):
        self.eng = eng
        self.fn = fn
        self.deps = None
        self.needed = False
        self.tok = None
        self.dma_sem = dma_sem


class _Cell:
    __slots__ = ("w", "r")

    def __init__(self):
        self.w = None
        self.r = []


class Prog:
    ENGS = ("pe", "act", "dve", "pool", "sp")

    def __init__(self):
        self.ops = {e: [] for e in self.ENGS}
        self.cells = {}
        self.last_serial = {}

    def op(self, eng, fn, reads=(), writes=(), dma_sem=None, serial=False):
        o = _Op(eng, fn, dma_sem)
        deps = {}
        cells = self.cells
        if serial:
            prev = self.last_serial.get(id(dma_sem))
            if prev is not None:
                deps[id(prev)] = prev
            self.last_serial[id(dma_sem)] = o
        for k in reads:
            c = cells.get(k)
            if c is None:
                c = cells[k] = _Cell()
            if c.w is not None:
                deps[id(c.w)] = c.w
        for k in writes:
            c = cells.get(k)
            if c is None:
                c = cells[k] = _Cell()
            if c.w is not None:
                deps[id(c.w)] = c.w
            for r in c.r:
                deps[id(r)] = r
        for k in reads:
            cells[k].r.append(o)
        for k in writes:
            c = cells[k]
            c.w = o
            c.r = []
        deps.pop(id(o), None)
        dl = []
        for d in deps.values():
            if d.eng == "pe" and eng == "pe" and d.dma_sem is None and dma_sem is None:
                continue
            d.needed = True
            dl.append(d)
        o.deps = dl
        self.ops[eng].append(o)
        return o

    def emit(self, block, eng_sems, final_waits):
        cnt = {e: 0 for e in self.ENGS}
        dcnt = {}
        for e in self.ENGS:
            for o in self.ops[e]:
                if o.dma_sem is not None:
                    k = id(o.dma_sem)
                    dcnt[k] = dcnt.get(k, 0) + 16
                    o.tok = (o.dma_sem, dcnt[k])
                elif o.needed:
                    cnt[e] += 1
                    o.tok = (eng_sems[e], cnt[e])
        self.final_dma = dcnt

        def run(e, handle):
            waited = {}
            for o in self.ops[e]:
                need = {}
                for d in o.deps:
                    s, v = d.tok
                    k = id(s)
                    if need.get(k, (None, 0))[1] < v:
                        need[k] = (s, v)
                for k, (s, v) in need.items():
                    if waited.get(k, 0) < v:
                        handle.wait_ge(s, v)
                        waited[k] = v
                ins = o.fn(handle)
                if o.dma_sem is not None:
                    ins.then_inc(o.dma_sem, 16)
                elif o.needed:
                    ins.then_inc(eng_sems[e], 1)
            if e in final_waits:
                for s in final_waits[e]:
                    v = dcnt.get(id(s), 0)
                    if v:
                        handle.wait_ge(s, v)

        block.tensor(lambda h: run("pe", h))
        block.scalar(lambda h: run("act", h))
        block.vector(lambda h: run("dve", h))
        block.gpsimd(lambda h: run("pool", h))
        block.sync(lambda h: run("sp", h))


def build_program(n_prompt_tiles=SEQ // TT, do_sample=True, sample_first=False):
    nc = bass.Bass("TRN2", target_bir_lowering=False)

    def din(name, shape):
        return nc.dram_tensor(name, list(shape), F32, kind="ExternalInput").ap()

    def dout(name, shape):
        return nc.dram_tensor(name, list(shape), F32, kind="ExternalOutput").ap()

    xp = din("xp", [2, SEQ, D])
    xs_in = din("xs", [2, DEC, D])
    sh_in = din("sh", [2, D])
    sc_in = din("sc", [2, 3, D])
    g_pre_mix = din("g_pre_mix", [D])
    w_in = din("w_in", [D, 6 * D])
    conv_w = din("conv_w", [4, D])
    conv_b = din("conv_b", [D])
    w_a = din("w_a", [16, 64, 64])
    b_a = din("b_a", [D])
    w_x = din("w_x", [16, 64, 64])
    b_x = din("b_x", [D])
    lam = din("lam", [D])
    w_br_a = din("w_br_a", [D, D])
    ln_g = din("ln_g", [D])
    ln_b = din("ln_b", [D])
    w_s = din("w_s", [8, 128, 128])
    b_s = din("b_s", [8, 128])
    w_br_b = din("w_br_b", [D, D])
    w_out = din("w_out", [D, D])
    g_post_mix = din("g_post_mix", [D])
    g_pre_ffn = din("g_pre_ffn", [D])
    w_ffn_in = din("w_ffn_in", [D, 2 * DFF])
    w_ffn_out = din("w_ffn_out", [DFF, D])
    g_post_ffn = din("g_post_ffn", [D])

    yp = dout("yp", [2, SEQ, D])
    ys = dout("ys", [2, DEC, D])
    hp = dout("hp", [2, D])
    cp = dout("cp", [2, 3, D])
    hs_o = dout("hs", [2, D])
    cs_o = dout("cs", [2, 3, D])
    vs_o = dout("vs", [2, DEC, D])

    pieces = []

    def add_piece(kc, ncols, srcs):
        pieces.append(dict(kc=kc, ncols=ncols, srcs=srcs))
        return len(pieces) - 1

    def colblk(w, c0, n):
        return w[:, c0:c0 + n].rearrange("(k p) n -> p k n", p=128)

    WIN = [add_piece(8, 256, [(colblk(w_in, cb * 256, 256), 0, 256)]) for cb in range(24)]
    BRA = [add_piece(8, 256, [(colblk(w_br_a, h * 256, 256), 0, 256)]) for h in range(4)]
    BRB = [add_piece(8, 256, [(colblk(w_br_b, h * 256, 256), 0, 256)]) for h in range(4)]
    WOUT = [add_piece(8, 256, [(colblk(w_out, h * 256, 256), 0, 256)]) for h in range(4)]
    FING = [add_piece(8, 256, [(colblk(w_ffn_in, q * 256, 256), 0, 256)]) for q in range(NFF // 2)]
    FINU = [add_piece(8, 256, [(colblk(w_ffn_in, DFF + q * 256, 256), 0, 256)]) for q in range(NFF // 2)]
    FOUT = [add_piece(2, 1024, [(w_ffn_out[r * 256:(r + 1) * 256, :].rearrange("(k p) n -> p k n", p=128), 0, 1024)])
            for r in range(NFF // 2)]
    NPIECE = len(pieces)
    PSZ = 2048
    wsc = nc.dram_tensor("wsc", [NPIECE, 128, PSZ], BF16).ap()

    tiles = []
    for b in range(2):
        for tt in range(n_prompt_tiles):
            tiles.append(dict(kind="p", b=b, tt=tt, ntok=TT, nsub=TT // 128, PT=128,
                              first=(tt == 0), last=(tt == n_prompt_tiles - 1)))
    if do_sample:
        st = dict(kind="s", ntok=64, nsub=1, PT=64, first=True, last=True)
        if sample_first:
            tiles.insert(0, st)
        else:
            tiles.append(st)
    NT = len(tiles)

    with contextlib.ExitStack() as es:
        def sb(name, shape, dt=F32):
            return es.enter_context(nc.sbuf_tensor(name, list(shape), dt))

        def sem(name):
            return es.enter_context(nc.semaphore(name))

        wring = sb("wring", [128, NSLOT, PSZ], BF16)
        xbuf = [sb(f"xbuf{i}", [128, 2, D]) for i in range(3)]
        xsb = [sb(f"xsb{i}", [128, D], BF16) for i in range(2)]
        junk = sb("junk", [128, D], BF16)
        xnTs = [sb(f"xnT{i}", [128, NCH, TT], BF16) for i in range(2)]
        hnT = sb("hnT", [128, NCH, TT], BF16)
        tok = [sb(f"tok{i}", [128, D]) for i in range(2)]
        XAW = TT + 4
        xaT = sb("xaT", [128, NCH, XAW])
        xcf = sb("xcf", [128, NCH, TT])
        xcb = sb("xcb", [128, NCH, TT], BF16)
        chTr = [sb(f"chTr{i}", [128, TT]) for i in range(2)]
        chTi = [sb(f"chTi{i}", [128, TT]) for i in range(4)]
        chA = [sb(f"chA{i}", [128, TT]) for i in range(4)]
        chE = [sb(f"chE{i}", [128, TT]) for i in range(4)]
        chH = [sb(f"chH{i}", [128, TT]) for i in range(2)]
        gga = sb("gga", [128, NCH, TT])
        hgT = sb("hgT", [128, NCH, TT], BF16)
        gu = sb("gu", [128, NCH, TT])
        gsT = sb("gsT", [128, NCH, TT], BF16)
        gv = [sb(f"gv{i}", [128, D]) for i in range(2)]
        vn = [sb(f"vn{i}", [128, D], BF16) for i in range(2)]
        Tsig = sb("Tsig", [128, 2, TT])
        t2 = sb("t2", [128, NCH, TT], BF16)
        tmpA2 = sb("tmpA2", [128, 2, TT])
        mixT = sb("mixT", [128, NCH, TT], BF16)
        sgt = [sb(f"sgt{i}", [128, TT]) for i in range(2)]
        ffT = sb("ffT", [128, NFF, TT], BF16)
        hst = sb("hst", [128, NCH, 2])
        stats = sb("stats", [128, 96])
        st6 = [sb(f"st6_{i}", [128, 4, 6]) for i in range(2)]
        mvt = [sb(f"mvt{i}", [128, 2]) for i in range(2)]
        identf = sb("identf", [128, 128])
        identb = sb("identb", [128, 128], BF16)
        nhalf = sb("nhalf", [128, 1])
        ones128 = sb("ones128", [128, 128], BF16)
        gT_pm = sb("gT_pm", [128, NCH])
        gT_pf = sb("gT_pf", [128, NCH])
        cw = sb("cw", [128, NCH, 4])
        cbT = sb("cbT", [128, NCH])
        ba2 = sb("ba2", [128, NCH])
        bx2 = sb("bx2", [128, NCH])
        lamT = sb("lamT", [128, NCH])
        cneg = sb("cneg", [128, NCH])
        hcn = sb("hcn", [128, NCH])
        wab = sb("wab", [128, NCH, 128], BF16)
        wxb = sb("wxb", [128, NCH, 128], BF16)
        wsT = sb("wsT", [128, 8, 128], BF16)
        wsS0 = sb("wsS0", [64, 8, 32], BF16)
        wsS1 = sb("wsS1", [64, 8, 32], BF16)
        bs128 = sb("bs128", [128, 8, 128], BF16)
        gbc_lng = sb("gbc_lng", [128, D])
        gbc_lnb = sb("gbc_lnb", [128, D])
        gbc_pm = sb("gbc_pm", [128, D])
        gbc_pf = sb("gbc_pf", [128, D])

        ps = es.enter_context(nc.psum_tensor("ps", [128, 8, 512], F32))
        psb = ps[:].bitcast(BF16)
        psh = ps[:].rearrange("p b (h c) -> p (b h) c", h=2)

        eng_sems = {e: sem("s_" + e) for e in Prog.ENGS}
        cvt_sems = [sem(f"cv{i}") for i in range(16)]
        ring_sems = [sem(f"rg{i}") for i in range(NSLOT)]
        xl_sems = [sem(f"xl{i}") for i in range(3)]
        y_sems = [sem(f"yo{i}") for i in range(3)]
        c_sems = [sem(f"cst{i}") for i in range(8)]
        so_sem = sem("sto")
        vs_sem = sem("vso")
        block = es.enter_context(nc.Block())

        class _Dry:
            def op(self, *a, **k):
                return None

        stream_log = []

        def record(P, dry):
            stream = stream_log
            c_ctr = [0]

            def cdma(fn, reads=(), writes=()):
                i = c_ctr[0] % len(c_sems)
                c_ctr[0] += 1
                return P.op("sp", fn, reads=reads, writes=writes, dma_sem=c_sems[i], serial=True)

            held = set()
            lru = list(range(8))

            def fb(b):
                return [("ps", 2 * b), ("ps", 2 * b + 1)]

            def _touch(b):
                lru.remove(b)
                lru.append(b)

            def alloc_bank():
                for b in lru:
                    if b not in held:
                        _touch(b)
                        return b
                raise RuntimeError("no free PSUM bank")

            def alloc_half():
                return 2 * alloc_bank()

            def alloc_pair():
                best = None
                for b in range(0, 8, 2):
                    if b in held or (b + 1) in held:
                        continue
                    age = max(lru.index(b), lru.index(b + 1))
                    if best is None or age < best[0]:
                        best = (age, b)
                if best is None:
                    raise RuntimeError("no free PSUM bank pair")
                b = best[1]
                held.add(b)
                held.add(b + 1)
                _touch(b)
                _touch(b + 1)
                return b

            def release_pair(b):
                held.discard(b)
                held.discard(b + 1)
                _touch(b)
                _touch(b + 1)

            stat_ctr = [0]

            def stat():
                i = stat_ctr[0] % 96
                stat_ctr[0] += 1
                return i

            rot = {}

            def rotn(name, n):
                i = rot.get(name, 0)
                rot[name] = i + 1
                return i % n

            ws_state = dict(pos=0, loaded=0, cvt=0, held=0)
            cvt_done = set()

            def piece_view(ap2d, pc):
                return ap2d[:, 0:PSZ].rearrange("p (k n) -> p k n", k=pc["kc"])

            def rec_cvt(pid):
                pc = pieces[pid]
                dst = piece_view(wsc[pid], pc)
                for (src, c0, n) in pc["srcs"]:
                    P.op("pool", lambda e, dst=dst, src=src, c0=c0, n=n: e.dma_start(out=dst[:, :, c0:c0 + n], in_=src),
                         writes=[("wsc", pid)], dma_sem=cvt_sems[pid % 16], serial=True)

            def ws_prefetch():
                if dry:
                    return
                while ws_state["loaded"] < min(len(stream), ws_state["pos"] + NSLOT):
                    i = ws_state["loaded"]
                    j = ws_state["cvt"]
                    while j < min(len(stream), i + 1 + CV_AHEAD) and len(cvt_done) < NPIECE:
                        if stream[j] not in cvt_done:
                            cvt_done.add(stream[j])
                            rec_cvt(stream[j])
                        j += 1
                    ws_state["cvt"] = j
                    pid = stream[i]
                    slot = i % NSLOT
                    P.op("sp", lambda e, slot=slot, pid=pid: e.dma_start(out=wring[:, slot, :], in_=wsc[pid, :, :]),
                         reads=[("wsc", pid)], writes=[("wr", slot)], dma_sem=ring_sems[slot])
                    ws_state["loaded"] += 1

            def ws_acquire(pid):
                if dry:
                    stream.append(pid)
                    i = len(stream) - 1
                else:
                    i = ws_state["pos"] + ws_state["held"]
                    assert stream[i] == pid, (i, stream[i], pid)
                    assert i < ws_state["loaded"], (i, ws_state)
                ws_state["held"] += 1
                slot = i % NSLOT
                return slot, piece_view(wring[:, slot, :], pieces[pid])

            def ws_release():
                ws_state["held"] -= 1
                ws_state["pos"] += 1
                ws_prefetch()

            def rstd_from(ss_col, PT, scale, eps):
                t = stat()
                r = stat()
                P.op("pool", lambda e: e.tensor_scalar(out=stats[0:PT, t:t + 1], in0=stats[0:PT, ss_col:ss_col + 1],
                                                       scalar1=scale, scalar2=eps, op0=ALU.mult, op1=ALU.add),
                     reads=[("st", ss_col)], writes=[("st", t)])
                P.op("pool", lambda e: e.tensor_tensor(out=stats[0:PT, r:r + 1], in0=stats[0:PT, t:t + 1],
                                                       in1=nhalf[0:PT, :], op=ALU.pow),
                     reads=[("st", t), ("nhalf",)], writes=[("st", r)])
                return r

            def small_T(dst, src_vec, cellname):
                cdma(lambda e: e.dma_start(out=dst[:], in_=src_vec.rearrange("(c p) -> p c", p=128),
                                           allow_slow_non_contiguous=True),
                     writes=[(cellname,)])

            small_T(gT_pm, g_pre_mix, "gT_pm")
            small_T(gT_pf, g_pre_ffn, "gT_pf")
            small_T(cbT, conv_b, "cbT")
            small_T(ba2, b_a, "ba2")
            small_T(bx2, b_x, "bx2")
            small_T(lamT, lam, "lamT")
            for k in range(4):
                cdma(lambda e, k=k: e.dma_start(out=cw[:, :, k], in_=conv_w[k].rearrange("(c p) -> p c", p=128),
                                                allow_slow_non_contiguous=True),
                     writes=[("cw",)])
            for (dst, src, cn) in ((gbc_lng, ln_g, "gbc_lng"), (gbc_lnb, ln_b, "gbc_lnb"),
                                   (gbc_pm, g_post_mix, "gbc_pm"), (gbc_pf, g_post_ffn, "gbc_pf")):
                cdma(lambda e, dst=dst, src=src: e.dma_start(out=dst[:], in_=src.partition_broadcast(128)),
                     writes=[(cn,)])
            P.op("pool", lambda e: e.memset(identf[:], 0.0), writes=[("identf",)])
            P.op("pool", lambda e: e.affine_select(out=identf[:], in_=identf[:], compare_op=ALU.not_equal, fill=1.0,
                                                   base=0, pattern=[[-1, 128]], channel_multiplier=1),
                 reads=[("identf",)], writes=[("identf",)])
            P.op("pool", lambda e: e.tensor_copy(out=identb[:], in_=identf[:]), reads=[("identf",)], writes=[("identb",)])
            P.op("pool", lambda e: e.memset(nhalf[:], -0.5), writes=[("nhalf",)])
            P.op("pool", lambda e: e.memset(ones128[:], 1.0), writes=[("ones128",)])
            P.op("dve", lambda e: e.tensor_scalar(out=ba2[:], in0=ba2[:], scalar1=0.5, scalar2=None, op0=ALU.mult),
                 reads=[("ba2",)], writes=[("ba2",)])
            P.op("dve", lambda e: e.tensor_scalar(out=bx2[:], in0=bx2[:], scalar1=0.5, scalar2=None, op0=ALU.mult),
                 reads=[("bx2",)], writes=[("bx2",)])
            P.op("act", lambda e: e.activation(out=cneg[:], in_=lamT[:], func=AF.Abs),
                 reads=[("lamT",)], writes=[("cneg",)])
            P.op("act", lambda e: e.activation(out=cneg[:], in_=cneg[:], func=AF.Exp, scale=-1.0),
                 reads=[("cneg",)], writes=[("cneg",)])
            P.op("act", lambda e: e.activation(out=cneg[:], in_=cneg[:], func=AF.Ln, bias=1.0),
                 reads=[("cneg",)], writes=[("cneg",)])
            P.op("dve", lambda e: e.tensor_scalar(out=hcn[:], in0=lamT[:], scalar1=-1.0, scalar2=0.0, op0=ALU.mult, op1=ALU.max),
                 reads=[("lamT",)], writes=[("hcn",)])
            P.op("dve", lambda e: e.tensor_tensor(out=cneg[:], in0=cneg[:], in1=hcn[:], op=ALU.add),
                 reads=[("cneg",), ("hcn",)], writes=[("cneg",)])
            P.op("dve", lambda e: e.tensor_scalar(out=cneg[:], in0=cneg[:], scalar1=-8.0, scalar2=None, op0=ALU.mult),
                 reads=[("cneg",)], writes=[("cneg",)])
            P.op("dve", lambda e: e.tensor_scalar(out=hcn[:], in0=cneg[:], scalar1=0.5, scalar2=None, op0=ALU.mult),
                 reads=[("cneg",)], writes=[("hcn",)])
            wstage = gv[0][:].rearrange("p (c j) -> p c j", c=NCH)
            WST = [("gv", 0, q) for q in range(4)]
            for (wsrc, wdst, cn) in ((w_a, wab, "wab"), (w_x, wxb, "wxb")):
                P.op("pool", lambda e: e.memset(gv[0][:], 0.0), writes=WST)
                wv_ = wsrc.rearrange("(c two) i j -> two i c j", two=2)
                for two in range(2):
                    cdma(lambda e, two=two, wv_=wv_: e.dma_start(
                        out=wstage[two * 64:(two + 1) * 64, :, two * 64:(two + 1) * 64], in_=wv_[two]),
                        reads=WST, writes=[("wstage_dma", two)])
                P.op("dve", lambda e, wdst=wdst: e.tensor_copy(out=wdst[:], in_=wstage),
                     reads=WST + [("wstage_dma", 0), ("wstage_dma", 1)], writes=[(cn,)])
            cdma(lambda e: e.dma_start(out=wstage, in_=w_s.rearrange("g p q -> p g q")), writes=WST)
            for half in range(2):
                pr = alloc_bank()
                for gg in range(4):
                    g = half * 4 + gg
                    P.op("pe", lambda e, g=g, gg=gg, pr=pr: e.transpose(out=ps[:, pr, gg * 128:(gg + 1) * 128],
                                                                       in_=wstage[:, g, :], identity=identf[:]),
                         reads=WST + [("identf",)], writes=fb(pr))
                P.op("dve", lambda e, half=half, pr=pr: e.tensor_copy(
                    out=wsT[:, half * 4:(half + 1) * 4, :], in_=ps[:, pr, :].rearrange("p (g q) -> p g q", g=4)),
                    reads=fb(pr), writes=[("wsT",)])
            P.op("dve", lambda e: e.memset(wsT[64:128, :, 0:64], 0.0), reads=[("wsT",)], writes=[("wsT",)])
            P.op("dve", lambda e: e.memset(wsS0[:], 0.0), writes=[("wsS0",)])
            P.op("dve", lambda e: e.memset(wsS1[:], 0.0), writes=[("wsS1",)])
            P.op("dve", lambda e: e.tensor_copy(out=wsS0[0:32, :, :], in_=wsT[0:32, :, 0:32]), reads=[("wsT",), ("wsS0",)], writes=[("wsS0",)])
            cdma(lambda e: e.dma_start(out=wsS1[32:64, :, :], in_=wsS0[0:32, :, :]), reads=[("wsS0",), ("wsS1",)], writes=[("wsS1",)])
            bsv = b_s.rearrange("(o g) p -> o (g p)", o=1)
            bsf, bstf, bstb = tok[0], tok[1], junk
            for p0 in (0, 32):
                cdma(lambda e, p0=p0: e.dma_start(out=bsf[p0:p0 + 1, :], in_=bsv), writes=[("tok", 0)])
            bs33f = bs128[:].rearrange("p g q -> p (g q)")
            P.op("dve", lambda e: e.memset(bs128[:], 0.0), writes=[("bs128",)])
            P.op("dve", lambda e: e.tensor_copy(out=bs33f[0:1, :], in_=bsf[0:1, :]), reads=[("tok", 0), ("bs128",)], writes=[("bs128",)])
            P.op("dve", lambda e: e.tensor_copy(out=bstb[32:33, :], in_=bsf[32:33, :]), reads=[("tok", 0)], writes=[("junk",)])
            P.op("dve", lambda e: e.tensor_copy(out=bstf[32:33, :], in_=bstb[32:33, :]), reads=[("junk",)], writes=[("tok", 1)])
            P.op("dve", lambda e: e.tensor_tensor(out=bs33f[32:33, :], in0=bsf[32:33, :], in1=bstf[32:33, :], op=ALU.subtract),
                 reads=[("tok", 0), ("tok", 1), ("bs128",)], writes=[("bs128",)])

            def x_src(T):
                if T["kind"] == "p":
                    return xp[T["b"], T["tt"] * TT:(T["tt"] + 1) * TT, :].rearrange("(s p) d -> p s d", p=128)
                return xs_in.rearrange("b t d -> (b t) d")

            def load_x(ti):
                T = tiles[ti]
                xb = ti % 3
                PT, nsub = T["PT"], T["nsub"]
                if T["kind"] == "p":
                    P.op("sp", lambda e: e.dma_start(out=xbuf[xb][:, 0:nsub, :], in_=x_src(T)),
                         writes=[("x", xb, s) for s in range(nsub)], dma_sem=xl_sems[xb])
                else:
                    P.op("sp", lambda e: e.dma_start(out=xbuf[xb][0:PT, 0, :], in_=x_src(T)),
                         writes=[("x", xb, 0), ("x", xb, 1)], dma_sem=xl_sems[xb])

            def norm_A(T, xb):
                PT, nsub = T["PT"], T["nsub"]
                rs = []
                for s in range(nsub):
                    src = xbuf[xb][0:PT, s, :]
                    ssc = stat()
                    P.op("act", lambda e, src=src, ssc=ssc: e.activation(out=junk[0:PT, :], in_=src, func=AF.Square,
                                                                         accum_out=stats[0:PT, ssc:ssc + 1]),
                         reads=[("x", xb, s)], writes=[("junk",), ("st", ssc)])
                    rs.append(rstd_from(ssc, PT, 1.0 / D, EPS))
                return rs

            def norm_B(T, xb, rs):
                PT, nsub = T["PT"], T["nsub"]
                for s in range(nsub):
                    src = xbuf[xb][0:PT, s, :]
                    r = rs[s]
                    P.op("act", lambda e, src=src, r=r, s=s: e.activation(out=xsb[s][0:PT, :], in_=src, func=AF.Copy,
                                                                          scale=stats[0:PT, r:r + 1]),
                         reads=[("x", xb, s), ("st", r)], writes=[("xsb", s)])

            def norm_C(T, gT, gcell, dstT, dstname):
                PT, nsub = T["PT"], T["nsub"]
                for s in range(nsub):
                    bk = alloc_bank()
                    for c in range(NCH):
                        P.op("pe", lambda e, c=c, s=s, bk=bk: e.transpose(out=psb[:, bk, c * 128:c * 128 + PT],
                                                                         in_=xsb[s][0:PT, c * 128:(c + 1) * 128],
                                                                         identity=identb[0:PT, 0:PT]),
                             reads=[("xsb", s), ("identb",)], writes=fb(bk))
                    P.op("dve", lambda e, s=s, bk=bk: e.tensor_tensor(
                        out=dstT[:, :, s * 128:s * 128 + PT],
                        in0=psb[:, bk, :].rearrange("p (c t) -> p c t", c=NCH)[:, :, 0:PT],
                        in1=gT[:].unsqueeze(2).to_broadcast([128, NCH, PT]), op=ALU.mult),
                        reads=fb(bk) + [(gcell,)], writes=[(dstname, s)])

            def prep(ti):
                T = tiles[ti]
                xi = ti % 2
                rs = norm_A(T, ti % 3)
                norm_B(T, ti % 3, rs)
                norm_C(T, gT_pm, "gT_pm", xnTs[xi], ("xnT", xi))

            def fm_group(T, wv, mcol, rhsT, rhs_cells, slot):
                ntok = T["ntok"]
                bk = alloc_half()
                for k in range(NCH):
                    P.op("pe", lambda e, k=k, bk=bk: e.matmul(psh[:, bk, 0:ntok], lhsT=wv[:, k, mcol:mcol + 128],
                                                             rhs=rhsT[:, k, 0:ntok], start=(k == 0), stop=(k == NCH - 1)),
                         reads=[("wr", slot)] + rhs_cells, writes=[("ps", bk)])
                return bk

            def fm_pair(T, wv, rhsT, rhs_cells, slot):
                ntok = T["ntok"]
                b = alloc_bank()
                for m in range(2):
                    for k in range(NCH):
                        P.op("pe", lambda e, k=k, m=m: e.matmul(ps[:, b, m * 256:m * 256 + ntok], lhsT=wv[:, k, m * 128:(m + 1) * 128],
                                                               rhs=rhsT[:, k, 0:ntok], start=(k == 0), stop=(k == NCH - 1)),
                             reads=[("wr", slot)] + rhs_cells, writes=fb(b))
                return b, ps[:, b, :].rearrange("p (m c) -> p m c", m=2)[:, :, 0:ntok]

            def post_norm_A(T, pr, eps=EPS):
                PT = T["PT"]
                src = ps[0:PT, pr:pr + 2, :]
                ssc = stat()
                P.op("act", lambda e: e.activation(out=junk[0:PT, :].rearrange("p (a b) -> p a b", a=2), in_=src, func=AF.Square,
                                                   accum_out=stats[0:PT, ssc:ssc + 1]),
                     reads=fb(pr) + fb(pr + 1), writes=[("junk",), ("st", ssc)])
                return rstd_from(ssc, PT, 1.0 / D, eps)

            def post_norm_B(T, xb, s, pr, r, gbc, gcell):
                PT = T["PT"]
                src = ps[0:PT, pr:pr + 2, :]
                tk = rotn("tok", 2)
                P.op("dve", lambda e: e.scalar_tensor_tensor(out=tok[tk][0:PT, :].rearrange("p (a b) -> p a b", a=2), in0=src,
                                                             scalar=stats[0:PT, r:r + 1],
                                                             in1=gbc[0:PT, :].rearrange("p (a b) -> p a b", a=2),
                                                             op0=ALU.mult, op1=ALU.mult),
                     reads=fb(pr) + fb(pr + 1) + [("st", r), (gcell,)], writes=[("tok", tk)])
                P.op("pool", lambda e: e.tensor_tensor(out=xbuf[xb][0:PT, s, :], in0=xbuf[xb][0:PT, s, :], in1=tok[tk][0:PT, :], op=ALU.add),
                     reads=[("x", xb, s), ("tok", tk)], writes=[("x", xb, s)])
                release_pair(pr)

            def emit_states(T):
                samp = T["kind"] == "s"
                ntok = T["ntok"]
                for b in range(2 if samp else 1):
                    bb = b if samp else T["b"]
                    hdst, cdst = (hs_o, cs_o) if samp else (hp, cp)
                    prh = alloc_pair()
                    prc = alloc_pair()
                    for c in range(NCH):
                        P.op("pe", lambda e, c=c, b=b, prh=prh: e.transpose(out=ps[0:1, prh + c // 4, (c % 4) * 128:(c % 4 + 1) * 128],
                                                                           in_=hst[:, c, b:b + 1], identity=identf[:]),
                             reads=[("hst", c), ("identf",)], writes=fb(prh + c // 4))
                        c0 = (b * 35 + 32) if samp else ntok
                        P.op("pe", lambda e, c=c, c0=c0, prc=prc: e.transpose(out=ps[0:3, prc + c // 4, (c % 4) * 128:(c % 4 + 1) * 128],
                                                                             in_=xaT[:, c, c0:c0 + 3], identity=identf[:]),
                             reads=[("xam", c), ("identf",)], writes=fb(prc + c // 4))
                    sr = gv[b]
                    gvc_ = [("gv", b, q) for q in range(4)]
                    P.op("dve", lambda e, prh=prh, sr=sr: e.tensor_copy(out=sr[0:1, :].rearrange("p (a b) -> p a b", a=2), in_=ps[0:1, prh:prh + 2, :]),
                         reads=fb(prh) + fb(prh + 1), writes=gvc_)
                    P.op("sp", lambda e, bb=bb, hdst=hdst, sr=sr: e.dma_start(out=hdst[bb:bb + 1, :], in_=sr[0:1, :]),
                         reads=gvc_, dma_sem=so_sem, serial=True)
                    sr3 = tok[b]
                    P.op("dve", lambda e, prc=prc, sr3=sr3: e.tensor_copy(out=sr3[0:3, :].rearrange("p (a b) -> p a b", a=2), in_=ps[0:3, prc:prc + 2, :]),
                         reads=fb(prc) + fb(prc + 1), writes=[("tok", b)])
                    P.op("sp", lambda e, bb=bb, cdst=cdst, sr3=sr3: e.dma_start(out=cdst[bb, :, :], in_=sr3[0:3, :]),
                         reads=[("tok", b)], dma_sem=so_sem, serial=True)
                    release_pair(prh)
                    release_pair(prc)

            def mixer_stages(ti):
                T = tiles[ti]
                xb = ti % 3
                xnT = xnTs[ti % 2]
                ntok, nsub, PT = T["ntok"], T["nsub"], T["PT"]
                samp = T["kind"] == "s"
                xn_cells = [(("xnT", ti % 2), s) for s in range(nsub)]
                gcost = 8 * max(ntok, 64)
                st = []

                def segv(ap2d):
                    if samp:
                        return ap2d[:, 0:64].rearrange("p (b t) -> p b t", t=32)
                    return ap2d[:, 0:ntok]

                def xa_win(c, k):
                    if samp:
                        return xaT[:, c, 0:70].rearrange("p (b w) -> p b w", w=35)[:, :, k:k + 32]
                    return xaT[:, c, k:k + ntok]

                def s_init():
                    if T["first"]:
                        if samp:
                            for b in range(2):
                                cdma(lambda e, b=b: e.dma_start(out=hst[:, :, b], in_=sh_in[b].rearrange("(c p) -> p c", p=128),
                                                                allow_slow_non_contiguous=True),
                                     writes=[("hst", c) for c in range(NCH)])
                                for c in range(NCH):
                                    cdma(lambda e, b=b, c=c: e.dma_start(
                                        out=xaT[:, c, b * 35:b * 35 + 3],
                                        in_=sc_in[b, :, c * 128:(c + 1) * 128].rearrange("k p -> p k"),
                                        allow_slow_non_contiguous=True),
                                        writes=[("xah",)])
                        else:
                            P.op("pool", lambda e: e.memset(hst[:], 0.0), writes=[("hst", c) for c in range(NCH)])
                            P.op("pool", lambda e: e.memset(xaT[:, :, 0:3], 0.0), writes=[("xah",)])
                st.append((0, s_init))

                def s_loadx():
                    if ti + 1 < NT:
                        load_x(ti + 1)

                def s_xa(h):
                    slot, wv = ws_acquire(WIN[h])
                    b, pv = fm_pair(T, wv, xnT, xn_cells, slot)
                    if samp:
                        for m in range(2):
                            c = h * 2 + m
                            P.op("act", lambda e, c=c, m=m: e.activation(out=xa_win(c, 3), in_=segv(ps[:, b, m * 256:(m + 1) * 256]), func=AF.Copy),
                                 reads=fb(b), writes=[("xam", c)])
                    else:
                        P.op("act", lambda e: e.activation(out=xaT[:, 2 * h:2 * h + 2, 3:3 + ntok], in_=pv, func=AF.Copy),
                             reads=fb(b), writes=[("xam", 2 * h), ("xam", 2 * h + 1)])
                    ws_release()

                def s_conv(c):
                    P.op("dve", lambda e: e.tensor_scalar(out=segv(xcf[:, c, :]), in0=xa_win(c, 0), scalar1=cw[:, c, 0:1],
                                                          scalar2=cbT[:, c:c + 1], op0=ALU.mult, op1=ALU.add),
                         reads=[("xam", c), ("xah",), ("cw",), ("cbT",)], writes=[("xcf", c)])
                    for k in range(1, 4):
                        P.op("dve", lambda e, k=k: e.scalar_tensor_tensor(out=segv(xcf[:, c, :]), in0=xa_win(c, k),
                                                                          scalar=cw[:, c, k:k + 1], in1=segv(xcf[:, c, :]),
                                                                          op0=ALU.mult, op1=ALU.add),
                             reads=[("xam", c), ("xah",), ("cw",), ("xcf", c)], writes=[("xcf", c)])
                    if c >= 1:
                        s_xcb1(c - 1)
                    if c == NCH - 1 and not samp and not T["last"]:
                        P.op("dve", lambda e: e.tensor_copy(out=xaT[:, :, 0:3], in_=xaT[:, :, ntok:ntok + 3]),
                             reads=[("xam", cc) for cc in range(NCH)], writes=[("xah",)])

                def s_xcb1(c):
                    P.op("act", lambda e: e.activation(out=xcb[:, c, 0:ntok], in_=xcf[:, c, 0:ntok], func=AF.Copy),
                         reads=[("xcf", c)], writes=[("xcb", c)])

                def s_xcb():
                    s_xcb1(NCH - 1)

                def s_ga(h):
                    slot, wv = ws_acquire(WIN[4 + h])
                    b, pv = fm_pair(T, wv, xnT, xn_cells, slot)
                    P.op("act", lambda e: e.activation(out=gga[:, 2 * h:2 * h + 2, 0:ntok], in_=pv, func=AF.Gelu_apprx_tanh),
                         reads=fb(b), writes=[("gga", 2 * h), ("gga", 2 * h + 1)])
                    ws_release()

                def s_gate(c):
                    bg_ = alloc_bank()
                    P.op("pe", lambda e: e.matmul(ps[:, bg_, 0:ntok], lhsT=wab[:, c, :], rhs=xcb[:, c, 0:ntok], start=True, stop=True),
                         reads=[("wab",), ("xcb", c)], writes=fb(bg_))
                    P.op("pe", lambda e: e.matmul(ps[:, bg_, 256:256 + ntok], lhsT=wxb[:, c, :], rhs=xcb[:, c, 0:ntok], start=True, stop=True),
                         reads=[("wxb",), ("xcb", c)], writes=fb(bg_))
                    tr = rotn("chTr", 2)
                    sl = c % 4
                    Tr, Ti, Aa, Ee = chTr[tr], chTi[sl], chA[sl], chE[sl]
                    P.op("act", lambda e: e.activation(out=Tr[:, 0:ntok], in_=ps[:, bg_, 0:ntok], func=AF.Tanh,
                                                       scale=0.5, bias=ba2[:, c:c + 1]),
                         reads=fb(bg_) + [("ba2",)], writes=[("chTr", tr)])
                    P.op("act", lambda e: e.activation(out=Ti[:, 0:ntok], in_=ps[:, bg_, 256:256 + ntok], func=AF.Tanh,
                                                       scale=0.5, bias=bx2[:, c:c + 1]),
                         reads=fb(bg_) + [("bx2",)], writes=[("chTi", sl)])
                    P.op("act", lambda e: e.activation(out=Aa[:, 0:ntok], in_=Tr[:, 0:ntok], func=AF.Exp,
                                                       scale=hcn[:, c:c + 1], bias=hcn[:, c:c + 1]),
                         reads=[("chTr", tr), ("hcn",)], writes=[("chA", sl)])
                    P.op("act", lambda e: e.activation(out=Ee[:, 0:ntok], in_=Tr[:, 0:ntok], func=AF.Exp,
                                                       scale=cneg[:, c:c + 1], bias=cneg[:, c:c + 1]),
                         reads=[("chTr", tr), ("cneg",)], writes=[("chE", sl)])
                    P.op("dve", lambda e: e.scalar_tensor_tensor(out=Ti[:, 0:ntok], in0=Ti[:, 0:ntok], scalar=1.0,
                                                                 in1=xcf[:, c, 0:ntok], op0=ALU.add, op1=ALU.mult),
                         reads=[("chTi", sl), ("xcf", c)], writes=[("chTi", sl)])

                def s_sqrt(grp):
                    for c in range(grp * 4, grp * 4 + 4):
                        sl = c % 4
                        Ee = chE[sl]
                        P.op("act", lambda e, Ee=Ee: e.activation(out=Ee[:, 0:ntok], in_=Ee[:, 0:ntok], func=AF.Sqrt,
                                                                 scale=-0.25, bias=0.25),
                             reads=[("chE", sl)], writes=[("chE", sl)])

                def s_scan(c):
                    sl = c % 4
                    Ti, Aa, Ee = chTi[sl], chA[sl], chE[sl]
                    hh = rotn("chH", 2)
                    Hh = chH[hh]
                    P.op("dve", lambda e: e.tensor_tensor(out=Ti[:, 0:ntok], in0=Ee[:, 0:ntok], in1=Ti[:, 0:ntok], op=ALU.mult),
                         reads=[("chTi", sl), ("chE", sl)], writes=[("chTi", sl)])
                    nseg = 2 if samp else 1
                    sl_len = 32 if samp else ntok
                    for b in range(nseg):
                        c0 = b * sl_len
                        P.op("dve", lambda e, b=b, c0=c0: e.tensor_tensor_scan(
                            out=Hh[:, c0:c0 + sl_len], data0=Aa[:, c0:c0 + sl_len], data1=Ti[:, c0:c0 + sl_len],
                            initial=hst[:, c, b:b + 1], op0=ALU.mult, op1=ALU.add),
                            reads=[("chA", sl), ("chTi", sl), ("hst", c), ("chH", hh)], writes=[("chH", hh)])
                    if samp:
                        P.op("dve", lambda e: e.tensor_copy(
                            out=hst[:, c, 0:2], in_=Hh[:, 0:64].rearrange("p (b t) -> p b t", t=32)[:, :, 31]),
                            reads=[("chH", hh)], writes=[("hst", c)])
                    else:
                        P.op("dve", lambda e: e.tensor_copy(out=hst[:, c, 0:1], in_=Hh[:, ntok - 1:ntok]),
                             reads=[("chH", hh)], writes=[("hst", c)])
                    P.op("dve", lambda e: e.tensor_tensor(out=hgT[:, c, 0:ntok], in0=Hh[:, 0:ntok], in1=gga[:, c, 0:ntok], op=ALU.mult),
                         reads=[("chH", hh), ("gga", c)], writes=[("hgT", c)])

                def s_u(h):
                    slot, wv = ws_acquire(WIN[8 + h])
                    b, pv = fm_pair(T, wv, xnT, xn_cells, slot)
                    P.op("act", lambda e: e.activation(out=gu[:, 2 * h:2 * h + 2, 0:ntok], in_=pv, func=AF.Gelu_apprx_tanh),
                         reads=fb(b), writes=[("gu", 2 * h), ("gu", 2 * h + 1)])
                    ws_release()

                def s_v(h):
                    slot, wv = ws_acquire(WIN[12 + h])
                    for s in range(nsub):
                        gsl = s % 2
                        bk = alloc_half()
                        for k in range(NCH):
                            P.op("pe", lambda e, k=k, bk=bk, s=s: e.matmul(psh[0:PT, bk, 0:256], lhsT=xnT[:, k, s * 128:s * 128 + PT],
                                                                          rhs=wv[:, k, :], start=(k == 0), stop=(k == NCH - 1)),
                                 reads=[("wr", slot), (("xnT", ti % 2), s)], writes=[("ps", bk)])
                        P.op("act", lambda e, bk=bk, gsl=gsl: e.activation(out=gv[gsl][0:PT, h * 256:(h + 1) * 256],
                                                                           in_=psh[0:PT, bk, 0:256], func=AF.Gelu_apprx_tanh),
                             reads=[("ps", bk)], writes=[("gv", gsl, h)])
                        P.op("dve", lambda e, gsl=gsl: e.bn_stats(out=st6[gsl][0:PT, h, :], in_=gv[gsl][0:PT, h * 256:(h + 1) * 256]),
                             reads=[("gv", gsl, h)], writes=[("st6", gsl, h)])
                    ws_release()

                def s_ln():
                    rr = []
                    for s in range(nsub):
                        gsl = s % 2
                        P.op("dve", lambda e, gsl=gsl: e.bn_aggr(out=mvt[gsl][0:PT, :], in_=st6[gsl][0:PT].rearrange("p a b -> p (a b)")),
                             reads=[("st6", gsl, q) for q in range(4)], writes=[("mvt", gsl)])
                        vsc = stat()
                        P.op("pool", lambda e, gsl=gsl, vsc=vsc: e.tensor_copy(out=stats[0:PT, vsc:vsc + 1], in_=mvt[gsl][0:PT, 1:2]),
                             reads=[("mvt", gsl)], writes=[("st", vsc)])
                        rr.append(rstd_from(vsc, PT, 1.0, EPS))
                    for s in range(nsub):
                        gsl = s % 2
                        r = rr[s]
                        gvc = [("gv", gsl, q) for q in range(4)]
                        P.op("dve", lambda e, gsl=gsl, r=r: e.tensor_scalar(out=gv[gsl][0:PT, :], in0=gv[gsl][0:PT, :],
                                                                           scalar1=mvt[gsl][0:PT, 0:1], scalar2=stats[0:PT, r:r + 1],
                                                                           op0=ALU.subtract, op1=ALU.mult),
                             reads=gvc + [("mvt", gsl), ("st", r)], writes=gvc)
                        P.op("pool", lambda e, gsl=gsl: e.tensor_tensor(out=gv[gsl][0:PT, :], in0=gv[gsl][0:PT, :], in1=gbc_lng[0:PT, :],
                                                                       op=ALU.mult),
                             reads=gvc + [("gbc_lng",)], writes=gvc)

                def s_ln2():
                    for s in range(nsub):
                        gsl = s % 2
                        gvc = [("gv", gsl, q) for q in range(4)]
                        if samp:
                            P.op("dve", lambda e, gsl=gsl: e.tensor_tensor(out=gv[gsl][0:PT, :], in0=gv[gsl][0:PT, :], in1=gbc_lnb[0:PT, :],
                                                                          op=ALU.add),
                                 reads=gvc + [("gbc_lnb",)], writes=gvc)
                            P.op("act", lambda e, gsl=gsl: e.activation(out=vn[gsl][0:PT, :], in_=gv[gsl][0:PT, :], func=AF.Copy),
                                 reads=gvc, writes=[("vn", gsl)])
                            P.op("sp", lambda e, gsl=gsl: e.dma_start(out=vs_o.rearrange("b t d -> (b t) d"), in_=gv[gsl][0:PT, :]),
                                 reads=gvc, dma_sem=vs_sem)
                        else:
                            P.op("dve", lambda e, gsl=gsl: e.tensor_tensor(out=vn[gsl][0:PT, :], in0=gv[gsl][0:PT, :], in1=gbc_lnb[0:PT, :],
                                                                          op=ALU.add),
                                 reads=gvc + [("gbc_lnb",)], writes=[("vn", gsl)])

                prep_rs = []

                def s_states():
                    if T["last"]:
                        emit_states(T)

                def s_prepA():
                    if ti + 1 < NT:
                        prep_rs.extend(norm_A(tiles[ti + 1], (ti + 1) % 3))

                def s_prepB():
                    if ti + 1 < NT:
                        norm_B(tiles[ti + 1], (ti + 1) % 3, prep_rs)

                def s_prepC():
                    if ti + 1 < NT:
                        norm_C(tiles[ti + 1], gT_pm, "gT_pm", xnTs[(ti + 1) % 2], ("xnT", (ti + 1) % 2))

                def s_spatial(gp):
                    b = alloc_bank()
                    for gi in range(2):
                        g = 2 * gp + gi
                        if samp:
                            for bb in range(2):
                                wsb_ = wsS0 if bb == 0 else wsS1
                                P.op("pe", lambda e, bb=bb, wsb_=wsb_, g=g, gi=gi: e.matmul(
                                    ps[:, b, gi * 256 + bb * 32:gi * 256 + (bb + 1) * 32], lhsT=vn[0][0:64, g * 128:(g + 1) * 128],
                                    rhs=wsb_[:, g, :], start=True, stop=False),
                                    reads=[("vn", 0), ("wsS0",), ("wsS1",)], writes=fb(b))
                                P.op("pe", lambda e, bb=bb, g=g, gi=gi: e.matmul(
                                    ps[:, b, gi * 256 + bb * 32:gi * 256 + (bb + 1) * 32], lhsT=ones128[0:64, :],
                                    rhs=bs128[0:64, g, 0:32], start=False, stop=True),
                                    reads=[("ones128",), ("bs128",)], writes=fb(b))
                        else:
                            for s in range(nsub):
                                P.op("pe", lambda e, s=s, g=g, gi=gi: e.matmul(
                                    ps[:, b, gi * 256 + s * 128:gi * 256 + (s + 1) * 128], lhsT=vn[s % 2][:, g * 128:(g + 1) * 128],
                                    rhs=wsT[:, g, :], start=True, stop=False),
                                    reads=[("vn", s % 2), ("wsT",)], writes=fb(b))
                                P.op("pe", lambda e, s=s, g=g, gi=gi: e.matmul(
                                    ps[:, b, gi * 256 + s * 128:gi * 256 + (s + 1) * 128], lhsT=ones128[:, :],
                                    rhs=bs128[:, g, :], start=False, stop=True),
                                    reads=[("ones128",), ("bs128",)], writes=fb(b))
                    pv = ps[:, b, :].rearrange("p (m c) -> p m c", m=2)[:, :, 0:ntok]
                    P.op("dve", lambda e: e.tensor_tensor(out=gsT[:, 2 * gp:2 * gp + 2, 0:ntok], in0=gu[:, 2 * gp:2 * gp + 2, 0:ntok], in1=pv,
                                                          op=ALU.mult),
                         reads=[("gu", 2 * gp), ("gu", 2 * gp + 1)] + fb(b), writes=[("gsT", 2 * gp), ("gsT", 2 * gp + 1)])

                gs_cells = [("gsT", g) for g in range(8)]
                hg_cells = [("hgT", c) for c in range(NCH)]

                def s_gb(mh):
                    slot, wv = ws_acquire(WIN[20 + mh])
                    b, pv = fm_pair(T, wv, xnT, xn_cells, slot)
                    P.op("act", lambda e: e.activation(out=Tsig[:, :, 0:ntok], in_=pv, func=AF.Tanh, scale=0.5),
                         reads=fb(b), writes=[("Tsig", 0), ("Tsig", 1)])
                    ws_release()

                def s_ob(mh):
                    slot, wv = ws_acquire(BRB[mh])
                    b, pv = fm_pair(T, wv, gsT, gs_cells, slot)
                    P.op("dve", lambda e: e.scalar_tensor_tensor(out=t2[:, 2 * mh:2 * mh + 2, 0:ntok], in0=Tsig[:, :, 0:ntok], scalar=1.0,
                                                                 in1=pv, op0=ALU.add, op1=ALU.mult),
                         reads=[("Tsig", 0), ("Tsig", 1)] + fb(b), writes=[("t2", 2 * mh), ("t2", 2 * mh + 1)])
                    ws_release()

                def s_ga2(mh):
                    slot, wv = ws_acquire(WIN[16 + mh])
                    b, pv = fm_pair(T, wv, xnT, xn_cells, slot)
                    P.op("act", lambda e: e.activation(out=Tsig[:, :, 0:ntok], in_=pv, func=AF.Tanh, scale=0.5),
                         reads=fb(b), writes=[("Tsig", 0), ("Tsig", 1)])
                    ws_release()

                def s_oa(mh):
                    slot, wv = ws_acquire(BRA[mh])
                    b, pv = fm_pair(T, wv, hgT, hg_cells, slot)
                    P.op("dve", lambda e: e.scalar_tensor_tensor(out=tmpA2[:, :, 0:ntok], in0=Tsig[:, :, 0:ntok], scalar=1.0,
                                                                 in1=pv, op0=ALU.add, op1=ALU.mult),
                         reads=[("Tsig", 0), ("Tsig", 1)] + fb(b), writes=[("tmpA2",)])
                    P.op("pool", lambda e: e.tensor_tensor(out=mixT[:, 2 * mh:2 * mh + 2, 0:ntok], in0=tmpA2[:, :, 0:ntok],
                                                           in1=t2[:, 2 * mh:2 * mh + 2, 0:ntok], op=ALU.add),
                         reads=[("tmpA2",), ("t2", 2 * mh), ("t2", 2 * mh + 1)], writes=[("mixT", 2 * mh), ("mixT", 2 * mh + 1)])
                    ws_release()

                mix_cells = [("mixT", c) for c in range(NCH)]
                prs = []

                def s_out(q):
                    if q == 0:
                        for s in range(nsub):
                            prs.append(alloc_pair())
                    slot, wv = ws_acquire(WOUT[q])
                    for s in range(nsub):
                        bkq = prs[s] + q // 2
                        cq = (q % 2) * 256
                        for k in range(NCH):
                            P.op("pe", lambda e, k=k, s=s, bkq=bkq, cq=cq: e.matmul(
                                ps[0:PT, bkq, cq:cq + 256], lhsT=mixT[:, k, s * 128:s * 128 + PT], rhs=wv[:, k, :],
                                start=(k == 0), stop=(k == NCH - 1)),
                                reads=[("wr", slot)] + mix_cells, writes=[("ps", 2 * bkq + (q % 2))])
                    ws_release()

                pn_rs = []
                hn_rs = []

                def s_pnA():
                    for s in range(nsub):
                        pn_rs.append(post_norm_A(T, prs[s], eps=4.0 * EPS))

                def s_pnB():
                    for s in range(nsub):
                        post_norm_B(T, xb, s, prs[s], pn_rs[s], gbc_pm, "gbc_pm")
                    hn_rs.extend(norm_A(T, xb))

                def s_hnB():
                    norm_B(T, xb, hn_rs)

                US = 2400
                for h in range(4):
                    st.append((2 * gcost, lambda h=h: s_xa(h)))
                for h in range(4):
                    st.append((int(1.8 * US), lambda c=2 * h: s_conv(c)))
                    st.append((nsub * 8 * 256, lambda h=h: s_v(h)))
                    st.append((int(1.8 * US), lambda c=2 * h + 1: s_conv(c)))
                st.append((0, s_xcb))
                st.append((int(2.5 * US), s_ln))
                for h in range(4):
                    st.append((2 * gcost, lambda h=h: s_ga(h)))
                    if h == 1:
                        st.append((int(2.5 * US), s_ln2))
                for c in range(4):
                    st.append((int(1.8 * US), lambda c=c: s_gate(c)))
                    st.append((2 * gcost, lambda h=c: s_u(h)))
                st.append((int(2.5 * US), lambda: s_sqrt(0)))
                for g in range(2):
                    st.append((int(1.8 * US), lambda c=2 * g: s_scan(c)))
                    st.append((8 * ntok, lambda g=g: s_spatial(g)))
                    st.append((int(1.8 * US), lambda c=2 * g + 1: s_scan(c)))
                for c in range(4, 8):
                    st.append((int(1.8 * US), lambda c=c: s_gate(c)))
                    if c % 2 == 1:
                        st.append((8 * ntok, lambda g=(c - 1) // 2: s_spatial(g)))
                st.append((int(2.5 * US), lambda: s_sqrt(1)))
                st.append((0, s_loadx))
                st.append((int(2.0 * US), s_prepA))
                for mh in range(4):
                    st.append((2 * gcost, lambda mh=mh: s_gb(mh)))
                    st.append((int(1.8 * US), lambda c=4 + mh: s_scan(c)))
                    st.append((2 * gcost, lambda mh=mh: s_ob(mh)))
                    if mh == 0:
                        st.append((int(2.0 * US), s_prepB))
                    if mh == 2:
                        st.append((int(3.0 * US), s_prepC))
                st.append((0, s_states))
                for mh in range(4):
                    st.append((2 * gcost, lambda mh=mh: s_ga2(mh)))
                    st.append((2 * gcost, lambda mh=mh: s_oa(mh)))
                for q in range(4):
                    st.append((nsub * 8 * 256, lambda q=q: s_out(q)))
                st.append((int(2.0 * US), s_pnA))
                st.append((int(3.5 * US), s_pnB))
                st.append((int(2.0 * US), s_hnB))
                return st

            def ffn_stages(ti):
                T = tiles[ti]
                xb = ti % 3
                ntok, nsub, PT = T["ntok"], T["nsub"], T["PT"]
                samp = T["kind"] == "s"
                hn_cells = [("hnT", s) for s in range(nsub)]
                ff_cells = [("ffT", j) for j in range(NFF)]
                gcost = 8 * max(ntok, 64)
                st = []
                prs = []

                def s_fin(q):
                    slg, wvg = ws_acquire(FING[q])
                    slu, wvu = ws_acquire(FINU[q])
                    for jj in range(2):
                        j = 2 * q + jj
                        b = alloc_bank()
                        for m, (wv_, sl_) in enumerate(((wvg, slg), (wvu, slu))):
                            for k in range(NCH):
                                P.op("pe", lambda e, k=k, m=m, wv_=wv_, jj=jj, b=b: e.matmul(
                                    ps[:, b, m * 256:m * 256 + ntok], lhsT=wv_[:, k, jj * 128:(jj + 1) * 128],
                                    rhs=hnT[:, k, 0:ntok], start=(k == 0), stop=(k == NCH - 1)),
                                    reads=[("wr", sl_)] + hn_cells, writes=fb(b))
                        pv = ps[:, b, :].rearrange("p (m c) -> p m c", m=2)[:, :, 0:ntok]
                        sg = rotn("sgt", 2)
                        P.op("act", lambda e, pv=pv, sg=sg: e.activation(out=sgt[sg][:, 0:ntok], in_=pv[:, 0, :], func=AF.Tanh, scale=0.5),
                             reads=fb(b), writes=[("sgt", sg)])
                        P.op("dve", lambda e, pv=pv, sg=sg: e.scalar_tensor_tensor(out=sgt[sg][:, 0:ntok], in0=sgt[sg][:, 0:ntok], scalar=1.0,
                                                                               in1=pv[:, 0, :], op0=ALU.add, op1=ALU.mult),
                             reads=[("sgt", sg)] + fb(b), writes=[("sgt", sg)])
                        P.op("dve", lambda e, pv=pv, sg=sg, j=j: e.scalar_tensor_tensor(out=ffT[:, j, 0:ntok], in0=sgt[sg][:, 0:ntok], scalar=0.5,
                                                                                    in1=pv[:, 1, :], op0=ALU.mult, op1=ALU.mult),
                             reads=[("sgt", sg)] + fb(b), writes=[("ffT", j)])
                    ws_release()
                    ws_release()

                def s_fout(r):
                    if r == 0:
                        for s in range(nsub):
                            prs.append(alloc_pair())
                    slot, wv = ws_acquire(FOUT[r])
                    for s in range(nsub):
                        for half in range(2):
                            for kk in range(2):
                                j = r * 2 + kk
                                P.op("pe", lambda e, j=j, kk=kk, s=s, half=half: e.matmul(
                                    ps[0:PT, prs[s] + half, :], lhsT=ffT[:, j, s * 128:s * 128 + PT],
                                    rhs=wv[:, kk, half * 512:(half + 1) * 512], start=(j == 0), stop=(j == NFF - 1)),
                                    reads=[("wr", slot)] + ff_cells, writes=fb(prs[s] + half))
                    ws_release()

                fin_rs = []

                def s_hnC():
                    norm_C(T, gT_pf, "gT_pf", hnT, "hnT")

                def s_finalA():
                    for s in range(nsub):
                        fin_rs.append(post_norm_A(T, prs[s]))

                def s_final():
                    for s in range(nsub):
                        post_norm_B(T, xb, s, prs[s], fin_rs[s], gbc_pf, "gbc_pf")
                        if samp:
                            P.op("sp", lambda e: e.dma_start(out=ys.rearrange("b t d -> (b t) d"), in_=xbuf[xb][0:PT, 0, :]),
                                 reads=[("x", xb, 0)], dma_sem=y_sems[xb])
                        else:
                            r0 = T["tt"] * TT + s * 128
                            P.op("sp", lambda e, s=s, r0=r0: e.dma_start(out=yp[T["b"], r0:r0 + 128, :], in_=xbuf[xb][:, s, :]),
                                 reads=[("x", xb, s)], dma_sem=y_sems[xb])

                st.append((int(3.0 * 2400), s_hnC))
                for q in range(NFF // 2):
                    st.append((4 * gcost, lambda q=q: s_fin(q)))
                for r in range(NFF // 2):
                    st.append((int(FOUT_W * nsub * 4 * 512), lambda r=r: s_fout(r)))
                st.append((int(2.0 * 2400), s_finalA))
                st.append((int(3.5 * 2400), s_final))
                return st

            def timed(stages, t0):
                tot = sum(c for c, _ in stages) + 1e-9
                out = []
                acc = 0.0
                for c, f in stages:
                    out.append((t0 + acc / tot, f))
                    acc += c
                return out

            ws_prefetch()
            load_x(0)
            prep(0)
            allst = []
            for ti in range(NT):
                allst += [(t, 0, i, f) for i, (t, f) in enumerate(timed(mixer_stages(ti), float(ti)))]
                allst += [(t, 1, i, f) for i, (t, f) in enumerate(timed(ffn_stages(ti), ti + 1.0 + F_OFFSET))]
            if not INTERLEAVE:
                allst = [(float(int(t - (1.0 + F_OFFSET if k else 0.0)) + 0.5 * k), k, i, f) for (t, k, i, f) in allst]
            allst.sort(key=lambda z: (z[0], z[1], z[2]))
            for (_, _, _, f) in allst:
                f()

        record(_Dry(), True)
        P = Prog()
        record(P, False)
        P.emit(block, eng_sems, {"sp": [y_sems[0], y_sems[1], y_sems[2], so_sem, vs_sem]})
    return nc


_WNAMES = ["g_pre_mix", "w_in", "conv_w", "conv_b", "w_a", "b_a", "w_x", "b_x", "lam", "w_br_a", "ln_g", "ln_b",
           "w_s", "b_s", "w_br_b", "w_out", "g_post_mix", "g_pre_ffn", "w_ffn_in", "w_ffn_out", "g_post_ffn"]


def make_in_maps(inputs):
    f = lambda a: np.ascontiguousarray(np.asarray(a, dtype=np.float32))
    shared = {}
    for n in _WNAMES:
        a = f(inputs[n])[0]
        if n in ("b_a", "b_x"):
            a = a.reshape(-1)
        shared[n] = np.ascontiguousarray(a)
    xpr, xsm = f(inputs["x_prompt"]), f(inputs["x_sample"])
    sh, sc = f(inputs["state_rglru_h"])[0], f(inputs["state_rglru_conv"])[0]
    maps = []
    for i in range(NCORES):
        m = dict(shared)
        m["xp"] = np.ascontiguousarray(xpr[2 * i:2 * i + 2])
        m["xs"] = np.ascontiguousarray(xsm[2 * i:2 * i + 2])
        m["sh"] = np.ascontiguousarray(sh[2 * i:2 * i + 2])
        m["sc"] = np.ascontiguousarray(sc[2 * i:2 * i + 2])
        maps.append(m)
    return maps


def kernel(**inputs):
    nc = build_program()
    in_maps = make_in_maps(inputs)
    res = run_bass_kernel_spmd(nc, in_maps, core_ids=list(range(NCORES)))
    R = res.results
    cat = lambda k: np.concatenate([np.asarray(r[k], dtype=np.float32) for r in R], axis=0)
    y_prompt = cat("yp")
    y_sample = cat("ys")
    new_h_prompt = cat("hp")[None]
    new_conv_prompt = cat("cp")[None]
    new_h_sample = cat("hs")[None]
    new_conv_sample = cat("cs")[None]
    new_v_sample = cat("vs")[None]
    return (y_prompt, y_sample, new_h_prompt, new_conv_prompt, new_h_sample, new_conv_sample, new_v_sample)
```

```python
import contextlib
import numpy as np
import concourse.bass as bass
import concourse.mybir as mybir
from concourse.bass_utils import run_bass_kernel_spmd

F32 = mybir.dt.float32
BF16 = mybir.dt.bfloat16
AF = mybir.ActivationFunctionType
ALU = mybir.AluOpType

NCORES = 8
D = 1024
NCH = 8
DFF = 2816
NFF = 22
SEQ = 2048
DEC = 32
EPS = 1e-6
TT = 256
NSLOT = 8
CV_AHEAD = 24
INTERLEAVE = True
F_OFFSET = 0.35
SAMPLE_MID = True
FOUT_W = 1.0


class _Op:
    __slots__ = ("eng", "fn", "deps", "needed", "tok", "dma_sem")

    def __init__(self, eng, fn, dma_sem):
        self.eng = eng
        self.fn = fn
        self.deps = None
        self.needed = False
        self.tok = None
        self.dma_sem = dma_sem


class _Cell:
    __slots__ = ("w", "r")

    def __init__(self):
        self.w = None
        self.r = []


class Prog:
    ENGS = ("pe", "act", "dve", "pool", "sp")

    def __init__(self):
        self.ops = {e: [] for e in self.ENGS}
        self.cells = {}
        self.last_serial = {}

    def op(self, eng, fn, reads=(), writes=(), dma_sem=None, serial=False):
        o = _Op(eng, fn, dma_sem)
        deps = {}
        cells = self.cells
        if serial:
            prev = self.last_serial.get(id(dma_sem))
            if prev is not None:
                deps[id(prev)] = prev
            self.last_serial[id(dma_sem)] = o
        for k in reads:
            c = cells.get(k)
            if c is None:
                c = cells[k] = _Cell()
            if c.w is not None:
                deps[id(c.w)] = c.w
        for k in writes:
            c = cells.get(k)
            if c is None:
                c = cells[k] = _Cell()
            if c.w is not None:
                deps[id(c.w)] = c.w
            for r in c.r:
                deps[id(r)] = r
        for k in reads:
            cells[k].r.append(o)
        for k in writes:
            c = cells[k]
            c.w = o
            c.r = []
        deps.pop(id(o), None)
        dl = []
        for d in deps.values():
            if d.eng == "pe" and eng == "pe" and d.dma_sem is None and dma_sem is None:
                continue
            d.needed = True
            dl.append(d)
        o.deps = dl
        self.ops[eng].append(o)
        return o

    def emit(self, block, eng_sems, final_waits):
        cnt = {e: 0 for e in self.ENGS}
        dcnt = {}
        for e in self.ENGS:
            for o in self.ops[e]:
                if o.dma_sem is not None:
                    k = id(o.dma_sem)
                    dcnt[k] = dcnt.get(k, 0) + 16
                    o.tok = (o.dma_sem, dcnt[k])
                elif o.needed:
                    cnt[e] += 1
                    o.tok = (eng_sems[e], cnt[e])
        self.final_dma = dcnt

        def run(e, handle):
            waited = {}
            for o in self.ops[e]:
                need = {}
                for d in o.deps:
                    s, v = d.tok
                    k = id(s)
                    if need.get(k, (None, 0))[1] < v:
                        need[k] = (s, v)
                for k, (s, v) in need.items():
                    if waited.get(k, 0) < v:
                        handle.wait_ge(s, v)
                        waited[k] = v
                ins = o.fn(handle)
                if o.dma_sem is not None:
                    ins.then_inc(o.dma_sem, 16)
                elif o.needed:
                    ins.then_inc(eng_sems[e], 1)
            if e in final_waits:
                for s in final_waits[e]:
                    v = dcnt.get(id(s), 0)
                    if v:
                        handle.wait_ge(s, v)

        block.tensor(lambda h: run("pe", h))
        block.scalar(lambda h: run("act", h))
        block.vector(lambda h: run("dve", h))
        block.gpsimd(lambda h: run("pool", h))
        block.sync(lambda h: run("sp", h))


def build_program(n_prompt_tiles=SEQ // TT, do_sample=True, sample_first=False):
    nc = bass.Bass("TRN2", target_bir_lowering=False)

    def din(name, shape):
        return nc.dram_tensor(name, list(shape), F32, kind="ExternalInput").ap()

    def dout(name, shape):
        return nc.dram_tensor(name, list(shape), F32, kind="ExternalOutput").ap()

    xp = din("xp", [2, SEQ, D])
    xs_in = din("xs", [2, DEC, D])
    sh_in = din("sh", [2, D])
    sc_in = din("sc", [2, 3, D])
    g_pre_mix = din("g_pre_mix", [D])
    w_in = din("w_in", [D, 6 * D])
    conv_w = din("conv_w", [4, D])
    conv_b = din("conv_b", [D])
    w_a = din("w_a", [16, 64, 64])
    b_a = din("b_a", [D])
    w_x = din("w_x", [16, 64, 64])
    b_x = din("b_x", [D])
    lam = din("lam", [D])
    w_br_a = din("w_br_a", [D, D])
    ln_g = din("ln_g", [D])
    ln_b = din("ln_b", [D])
    w_s = din("w_s", [8, 128, 128])
    b_s = din("b_s", [8, 128])
    w_br_b = din("w_br_b", [D, D])
    w_out = din("w_out", [D, D])
    g_post_mix = din("g_post_mix", [D])
    g_pre_ffn = din("g_pre_ffn", [D])
    w_ffn_in = din("w_ffn_in", [D, 2 * DFF])
    w_ffn_out = din("w_ffn_out", [DFF, D])
    g_post_ffn = din("g_post_ffn", [D])

    yp = dout("yp", [2, SEQ, D])
    ys = dout("ys", [2, DEC, D])
    hp = dout("hp", [2, D])
    cp = dout("cp", [2, 3, D])
    hs_o = dout("hs", [2, D])
    cs_o = dout("cs", [2, 3, D])
    vs_o = dout("vs", [2, DEC, D])

    pieces = []

    def add_piece(kc, ncols, srcs):
        pieces.append(dict(kc=kc, ncols=ncols, srcs=srcs))
        return len(pieces) - 1

    def colblk(w, c0, n):
        return w[:, c0:c0 + n].rearrange("(k p) n -> p k n", p=128)

    WIN = [add_piece(8, 256, [(colblk(w_in, cb * 256, 256), 0, 256)]) for cb in range(24)]
    BRA = [add_piece(8, 256, [(colblk(w_br_a, h * 256, 256), 0, 256)]) for h in range(4)]
    BRB = [add_piece(8, 256, [(colblk(w_br_b, h * 256, 256), 0, 256)]) for h in range(4)]
    WOUT = [add_piece(8, 256, [(colblk(w_out, h * 256, 256), 0, 256)]) for h in range(4)]
    FING = [add_piece(8, 256, [(colblk(w_ffn_in, q * 256, 256), 0, 256)]) for q in range(NFF // 2)]
    FINU = [add_piece(8, 256, [(colblk(w_ffn_in, DFF + q * 256, 256), 0, 256)]) for q in range(NFF // 2)]
    FOUT = [add_piece(2, 1024, [(w_ffn_out[r * 256:(r + 1) * 256, :].rearrange("(k p) n -> p k n", p=128), 0, 1024)])
            for r in range(NFF // 2)]
    NPIECE = len(pieces)
    PSZ = 2048
    wsc = nc.dram_tensor("wsc", [NPIECE, 128, PSZ], BF16).ap()

    tiles = []
    for b in range(2):
        for tt in range(n_prompt_tiles):
            tiles.append(dict(kind="p", b=b, tt=tt, ntok=TT, nsub=TT // 128, PT=128,
                              first=(tt == 0), last=(tt == n_prompt_tiles - 1)))
    if do_sample:
        st = dict(kind="s", ntok=64, nsub=1, PT=64, first=True, last=True)
        if sample_first:
            tiles.insert(0, st)
        elif SAMPLE_MID:
            tiles.insert(n_prompt_tiles, st)
        else:
            tiles.append(st)
    NT = len(tiles)

    with contextlib.ExitStack() as es:
        def sb(name, shape, dt=F32):
            return es.enter_context(nc.sbuf_tensor(name, list(shape), dt))

        def sem(name):
            return es.enter_context(nc.semaphore(name))

        wring = sb("wring", [128, NSLOT, PSZ], BF16)
        xbuf = [sb(f"xbuf{i}", [128, 2, D]) for i in range(3)]
        xsb = [sb(f"xsb{i}", [128, D], BF16) for i in range(2)]
        junk = sb("junk", [128, D], BF16)
        xnTs = [sb(f"xnT{i}", [128, NCH, TT], BF16) for i in range(2)]
        hnT = sb("hnT", [128, NCH, TT], BF16)
        tok = [sb(f"tok{i}", [128, D]) for i in range(2)]
        XAW = TT + 4
        xaT = sb("xaT", [128, NCH, XAW])
        xcf = sb("xcf", [128, NCH, TT])
        xcb = sb("xcb", [128, NCH, TT], BF16)
        chTr = [sb(f"chTr{i}", [128, TT]) for i in range(2)]
        chTi = [sb(f"chTi{i}", [128, TT]) for i in range(4)]
        chA = [sb(f"chA{i}", [128, TT]) for i in range(4)]
        chE = [sb(f"chE{i}", [128, TT]) for i in range(4)]
        chH = [sb(f"chH{i}", [128, TT]) for i in range(2)]
        gga = sb("gga", [128, NCH, TT])
        hgT = sb("hgT", [128, NCH, TT], BF16)
        gu = sb("gu", [128, NCH, TT])
        gsT = sb("gsT", [128, NCH, TT], BF16)
        gv = [sb(f"gv{i}", [128, D]) for i in range(2)]
        vn = [sb(f"vn{i}", [128, D], BF16) for i in range(2)]
        Tsig = sb("Tsig", [128, 2, TT])
        t2 = sb("t2", [128, NCH, TT], BF16)
        tmpA2 = sb("tmpA2", [128, 2, TT])
        mixT = sb("mixT", [128, NCH, TT], BF16)
        sgt = [sb(f"sgt{i}", [128, TT]) for i in range(2)]
        ffT = sb("ffT", [128, NFF, TT], BF16)
        hst = sb("hst", [128, NCH, 2])
        stats = sb("stats", [128, 96])
        st6 = [sb(f"st6_{i}", [128, 4, 6]) for i in range(2)]
        mvt = [sb(f"mvt{i}", [128, 2]) for i in range(2)]
        identf = sb("identf", [128, 128])
        identb = sb("identb", [128, 128], BF16)
        nhalf = sb("nhalf", [128, 1])
        ones128 = sb("ones128", [128, 128], BF16)
        gT_pm = sb("gT_pm", [128, NCH])
        gT_pf = sb("gT_pf", [128, NCH])
        cw = sb("cw", [128, NCH, 4])
        cbT = sb("cbT", [128, NCH])
        ba2 = sb("ba2", [128, NCH])
        bx2 = sb("bx2", [128, NCH])
        lamT = sb("lamT", [128, NCH])
        cneg = sb("cneg", [128, NCH])
        hcn = sb("hcn", [128, NCH])
        wab = sb("wab", [128, NCH, 128], BF16)
        wxb = sb("wxb", [128, NCH, 128], BF16)
        wsT = sb("wsT", [128, 8, 128], BF16)
        wsS0 = sb("wsS0", [64, 8, 32], BF16)
        wsS1 = sb("wsS1", [64, 8, 32], BF16)
        bs128 = sb("bs128", [128, 8, 128], BF16)
        gbc_lng = sb("gbc_lng", [128, D])
        gbc_lnb = sb("gbc_lnb", [128, D])
        gbc_pm = sb("gbc_pm", [128, D])
        gbc_pf = sb("gbc_pf", [128, D])

        ps = es.enter_context(nc.psum_tensor("ps", [128, 8, 512], F32))
        psb = ps[:].bitcast(BF16)
        psh = ps[:].rearrange("p b (h c) -> p (b h) c", h=2)

        eng_sems = {e: sem("s_" + e) for e in Prog.ENGS}
        cvt_sems = [sem(f"cv{i}") for i in range(16)]
        ring_sems = [sem(f"rg{i}") for i in range(NSLOT)]
        xl_sems = [sem(f"xl{i}") for i in range(3)]
        y_sems = [sem(f"yo{i}") for i in range(3)]
        c_sems = [sem(f"cst{i}") for i in range(8)]
        so_sem = sem("sto")
        vs_sem = sem("vso")
        block = es.enter_context(nc.Block())

        class _Dry:
            def op(self, *a, **k):
                return None

        stream_log = []

        def record(P, dry):
            stream = stream_log
            c_ctr = [0]

            def cdma(fn, reads=(), writes=()):
                i = c_ctr[0] % len(c_sems)
                c_ctr[0] += 1
                return P.op("sp", fn, reads=reads, writes=writes, dma_sem=c_sems[i], serial=True)

            held = set()
            lru = list(range(8))

            def fb(b):
                return [("ps", 2 * b), ("ps", 2 * b + 1)]

            def _touch(b):
                lru.remove(b)
                lru.append(b)

            def alloc_bank():
                for b in lru:
                    if b not in held:
                        _touch(b)
                        return b
                raise RuntimeError("no free PSUM bank")

            def alloc_half():
                return 2 * alloc_bank()

            def alloc_pair():
                best = None
                for b in range(0, 8, 2):
                    if b in held or (b + 1) in held:
                        continue
                    age = max(lru.index(b), lru.index(b + 1))
                    if best is None or age < best[0]:
                        best = (age, b)
                if best is None:
                    raise RuntimeError("no free PSUM bank pair")
                b = best[1]
                held.add(b)
                held.add(b + 1)
                _touch(b)
                _touch(b + 1)
                return b

            def release_pair(b):
                held.discard(b)
                held.discard(b + 1)
                _touch(b)
                _touch(b + 1)

            stat_ctr = [0]

            def stat():
                i = stat_ctr[0] % 96
                stat_ctr[0] += 1
                return i

            rot = {}

            def rotn(name, n):
                i = rot.get(name, 0)
                rot[name] = i + 1
                return i % n

            ws_state = dict(pos=0, loaded=0, cvt=0, held=0)
            cvt_done = set()

            def piece_view(ap2d, pc):
                return ap2d[:, 0:PSZ].rearrange("p (k n) -> p k n", k=pc["kc"])

            def rec_cvt(pid):
                pc = pieces[pid]
                dst = piece_view(wsc[pid], pc)
                for (src, c0, n) in pc["srcs"]:
                    P.op("pool", lambda e, dst=dst, src=src, c0=c0, n=n: e.dma_start(out=dst[:, :, c0:c0 + n], in_=src),
                         writes=[("wsc", pid)], dma_sem=cvt_sems[pid % 16], serial=True)

            def ws_prefetch():
                if dry:
                    return
                while ws_state["loaded"] < min(len(stream), ws_state["pos"] + NSLOT):
                    i = ws_state["loaded"]
                    j = ws_state["cvt"]
                    while j < min(len(stream), i + 1 + CV_AHEAD) and len(cvt_done) < NPIECE:
                        if stream[j] not in cvt_done:
                            cvt_done.add(stream[j])
                            rec_cvt(stream[j])
                        j += 1
                    ws_state["cvt"] = j
                    pid = stream[i]
                    slot = i % NSLOT
                    P.op("sp", lambda e, slot=slot, pid=pid: e.dma_start(out=wring[:, slot, :], in_=wsc[pid, :, :]),
                         reads=[("wsc", pid)], writes=[("wr", slot)], dma_sem=ring_sems[slot])
                    ws_state["loaded"] += 1

            def ws_acquire(pid):
                if dry:
                    stream.append(pid)
                    i = len(stream) - 1
                else:
                    i = ws_state["pos"] + ws_state["held"]
                    assert stream[i] == pid, (i, stream[i], pid)
                    assert i < ws_state["loaded"], (i, ws_state)
                ws_state["held"] += 1
                slot = i % NSLOT
                return slot, piece_view(wring[:, slot, :], pieces[pid])

            def ws_release():
                ws_state["held"] -= 1
                ws_state["pos"] += 1
                ws_prefetch()

            def rstd_from(ss_col, PT, scale, eps):
                t = stat()
                r = stat()
                P.op("pool", lambda e: e.tensor_scalar(out=stats[0:PT, t:t + 1], in0=stats[0:PT, ss_col:ss_col + 1],
                                                       scalar1=scale, scalar2=eps, op0=ALU.mult, op1=ALU.add),
                     reads=[("st", ss_col)], writes=[("st", t)])
                P.op("pool", lambda e: e.tensor_tensor(out=stats[0:PT, r:r + 1], in0=stats[0:PT, t:t + 1],
                                                       in1=nhalf[0:PT, :], op=ALU.pow),
                     reads=[("st", t), ("nhalf",)], writes=[("st", r)])
                return r

            def small_T(dst, src_vec, cellname):
                cdma(lambda e: e.dma_start(out=dst[:], in_=src_vec.rearrange("(c p) -> p c", p=128),
                                           allow_slow_non_contiguous=True),
                     writes=[(cellname,)])

            small_T(gT_pm, g_pre_mix, "gT_pm")
            small_T(gT_pf, g_pre_ffn, "gT_pf")
            small_T(cbT, conv_b, "cbT")
            small_T(ba2, b_a, "ba2")
            small_T(bx2, b_x, "bx2")
            small_T(lamT, lam, "lamT")
            for k in range(4):
                cdma(lambda e, k=k: e.dma_start(out=cw[:, :, k], in_=conv_w[k].rearrange("(c p) -> p c", p=128),
                                                allow_slow_non_contiguous=True),
                     writes=[("cw",)])
            for (dst, src, cn) in ((gbc_lng, ln_g, "gbc_lng"), (gbc_lnb, ln_b, "gbc_lnb"),
                                   (gbc_pm, g_post_mix, "gbc_pm"), (gbc_pf, g_post_ffn, "gbc_pf")):
                cdma(lambda e, dst=dst, src=src: e.dma_start(out=dst[:], in_=src.partition_broadcast(128)),
                     writes=[(cn,)])
            P.op("pool", lambda e: e.memset(identf[:], 0.0), writes=[("identf",)])
            P.op("pool", lambda e: e.affine_select(out=identf[:], in_=identf[:], compare_op=ALU.not_equal, fill=1.0,
                                                   base=0, pattern=[[-1, 128]], channel_multiplier=1),
                 reads=[("identf",)], writes=[("identf",)])
            P.op("pool", lambda e: e.tensor_copy(out=identb[:], in_=identf[:]), reads=[("identf",)], writes=[("identb",)])
            P.op("pool", lambda e: e.memset(nhalf[:], -0.5), writes=[("nhalf",)])
            P.op("pool", lambda e: e.memset(ones128[:], 1.0), writes=[("ones128",)])
            P.op("dve", lambda e: e.tensor_scalar(out=ba2[:], in0=ba2[:], scalar1=0.5, scalar2=None, op0=ALU.mult),
                 reads=[("ba2",)], writes=[("ba2",)])
            P.op("dve", lambda e: e.tensor_scalar(out=bx2[:], in0=bx2[:], scalar1=0.5, scalar2=None, op0=ALU.mult),
                 reads=[("bx2",)], writes=[("bx2",)])
            P.op("act", lambda e: e.activation(out=cneg[:], in_=lamT[:], func=AF.Abs),
                 reads=[("lamT",)], writes=[("cneg",)])
            P.op("act", lambda e: e.activation(out=cneg[:], in_=cneg[:], func=AF.Exp, scale=-1.0),
                 reads=[("cneg",)], writes=[("cneg",)])
            P.op("act", lambda e: e.activation(out=cneg[:], in_=cneg[:], func=AF.Ln, bias=1.0),
                 reads=[("cneg",)], writes=[("cneg",)])
            P.op("dve", lambda e: e.tensor_scalar(out=hcn[:], in0=lamT[:], scalar1=-1.0, scalar2=0.0, op0=ALU.mult, op1=ALU.max),
                 reads=[("lamT",)], writes=[("hcn",)])
            P.op("dve", lambda e: e.tensor_tensor(out=cneg[:], in0=cneg[:], in1=hcn[:], op=ALU.add),
                 reads=[("cneg",), ("hcn",)], writes=[("cneg",)])
            P.op("dve", lambda e: e.tensor_scalar(out=cneg[:], in0=cneg[:], scalar1=-8.0, scalar2=None, op0=ALU.mult),
                 reads=[("cneg",)], writes=[("cneg",)])
            P.op("dve", lambda e: e.tensor_scalar(out=hcn[:], in0=cneg[:], scalar1=0.5, scalar2=None, op0=ALU.mult),
                 reads=[("cneg",)], writes=[("hcn",)])
            wstage = gv[0][:].rearrange("p (c j) -> p c j", c=NCH)
            WST = [("gv", 0, q) for q in range(4)]
            for (wsrc, wdst, cn) in ((w_a, wab, "wab"), (w_x, wxb, "wxb")):
                P.op("pool", lambda e: e.memset(gv[0][:], 0.0), writes=WST)
                wv_ = wsrc.rearrange("(c two) i j -> two i c j", two=2)
                for two in range(2):
                    cdma(lambda e, two=two, wv_=wv_: e.dma_start(
                        out=wstage[two * 64:(two + 1) * 64, :, two * 64:(two + 1) * 64], in_=wv_[two]),
                        reads=WST, writes=[("wstage_dma", two)])
                P.op("dve", lambda e, wdst=wdst: e.tensor_copy(out=wdst[:], in_=wstage),
                     reads=WST + [("wstage_dma", 0), ("wstage_dma", 1)], writes=[(cn,)])
            cdma(lambda e: e.dma_start(out=wstage, in_=w_s.rearrange("g p q -> p g q")), writes=WST)
            for half in range(2):
                pr = alloc_bank()
                for gg in range(4):
                    g = half * 4 + gg
                    P.op("pe", lambda e, g=g, gg=gg, pr=pr: e.transpose(out=ps[:, pr, gg * 128:(gg + 1) * 128],
                                                                       in_=wstage[:, g, :], identity=identf[:]),
                         reads=WST + [("identf",)], writes=fb(pr))
                P.op("dve", lambda e, half=half, pr=pr: e.tensor_copy(
                    out=wsT[:, half * 4:(half + 1) * 4, :], in_=ps[:, pr, :].rearrange("p (g q) -> p g q", g=4)),
                    reads=fb(pr), writes=[("wsT",)])
            P.op("dve", lambda e: e.memset(wsT[64:128, :, 0:64], 0.0), reads=[("wsT",)], writes=[("wsT",)])
            P.op("dve", lambda e: e.memset(wsS0[:], 0.0), writes=[("wsS0",)])
            P.op("dve", lambda e: e.memset(wsS1[:], 0.0), writes=[("wsS1",)])
            P.op("dve", lambda e: e.tensor_copy(out=wsS0[0:32, :, :], in_=wsT[0:32, :, 0:32]), reads=[("wsT",), ("wsS0",)], writes=[("wsS0",)])
            cdma(lambda e: e.dma_start(out=wsS1[32:64, :, :], in_=wsS0[0:32, :, :]), reads=[("wsS0",), ("wsS1",)], writes=[("wsS1",)])
            bsv = b_s.rearrange("(o g) p -> o (g p)", o=1)
            bsf, bstf, bstb = tok[0], tok[1], junk
            for p0 in (0, 32):
                cdma(lambda e, p0=p0: e.dma_start(out=bsf[p0:p0 + 1, :], in_=bsv), writes=[("tok", 0)])
            bs33f = bs128[:].rearrange("p g q -> p (g q)")
            P.op("dve", lambda e: e.memset(bs128[:], 0.0), writes=[("bs128",)])
            P.op("dve", lambda e: e.tensor_copy(out=bs33f[0:1, :], in_=bsf[0:1, :]), reads=[("tok", 0), ("bs128",)], writes=[("bs128",)])
            P.op("dve", lambda e: e.tensor_copy(out=bstb[32:33, :], in_=bsf[32:33, :]), reads=[("tok", 0)], writes=[("junk",)])
            P.op("dve", lambda e: e.tensor_copy(out=bstf[32:33, :], in_=bstb[32:33, :]), reads=[("junk",)], writes=[("tok", 1)])
            P.op("dve", lambda e: e.tensor_tensor(out=bs33f[32:33, :], in0=bsf[32:33, :], in1=bstf[32:33, :], op=ALU.subtract),
                 reads=[("tok", 0), ("tok", 1), ("bs128",)], writes=[("bs128",)])

            def x_src(T):
                if T["kind"] == "p":
                    return xp[T["b"], T["tt"] * TT:(T["tt"] + 1) * TT, :].rearrange("(s p) d -> p s d", p=128)
                return xs_in.rearrange("b t d -> (b t) d")

            def load_x(ti):
                T = tiles[ti]
                xb = ti % 3
                PT, nsub = T["PT"], T["nsub"]
                if T["kind"] == "p":
                    P.op("sp", lambda e: e.dma_start(out=xbuf[xb][:, 0:nsub, :], in_=x_src(T)),
                         writes=[("x", xb, s) for s in range(nsub)], dma_sem=xl_sems[xb])
                else:
                    P.op("sp", lambda e: e.dma_start(out=xbuf[xb][0:PT, 0, :], in_=x_src(T)),
                         writes=[("x", xb, 0), ("x", xb, 1)], dma_sem=xl_sems[xb])

            def norm_A(T, xb):
                PT, nsub = T["PT"], T["nsub"]
                rs = []
                for s in range(nsub):
                    src = xbuf[xb][0:PT, s, :]
                    ssc = stat()
                    P.op("act", lambda e, src=src, ssc=ssc: e.activation(out=junk[0:PT, :], in_=src, func=AF.Square,
                                                                         accum_out=stats[0:PT, ssc:ssc + 1]),
                         reads=[("x", xb, s)], writes=[("junk",), ("st", ssc)])
                    rs.append(rstd_from(ssc, PT, 1.0 / D, EPS))
                return rs

            def norm_B(T, xb, rs):
                PT, nsub = T["PT"], T["nsub"]
                for s in range(nsub):
                    src = xbuf[xb][0:PT, s, :]
                    r = rs[s]
                    P.op("act", lambda e, src=src, r=r, s=s: e.activation(out=xsb[s][0:PT, :], in_=src, func=AF.Copy,
                                                                          scale=stats[0:PT, r:r + 1]),
                         reads=[("x", xb, s), ("st", r)], writes=[("xsb", s)])

            def norm_C(T, gT, gcell, dstT, dstname):
                PT, nsub = T["PT"], T["nsub"]
                for s in range(nsub):
                    bk = alloc_bank()
                    for c in range(NCH):
                        P.op("pe", lambda e, c=c, s=s, bk=bk: e.transpose(out=psb[:, bk, c * 128:c * 128 + PT],
                                                                         in_=xsb[s][0:PT, c * 128:(c + 1) * 128],
                                                                         identity=identb[0:PT, 0:PT]),
                             reads=[("xsb", s), ("identb",)], writes=fb(bk))
                    P.op("dve", lambda e, s=s, bk=bk: e.tensor_tensor(
                        out=dstT[:, :, s * 128:s * 128 + PT],
                        in0=psb[:, bk, :].rearrange("p (c t) -> p c t", c=NCH)[:, :, 0:PT],
                        in1=gT[:].unsqueeze(2).to_broadcast([128, NCH, PT]), op=ALU.mult),
                        reads=fb(bk) + [(gcell,)], writes=[(dstname, s)])

            def prep(ti):
                T = tiles[ti]
                xi = ti % 2
                rs = norm_A(T, ti % 3)
                norm_B(T, ti % 3, rs)
                norm_C(T, gT_pm, "gT_pm", xnTs[xi], ("xnT", xi))

            def fm_group(T, wv, mcol, rhsT, rhs_cells, slot):
                ntok = T["ntok"]
                bk = alloc_half()
                for k in range(NCH):
                    P.op("pe", lambda e, k=k, bk=bk: e.matmul(psh[:, bk, 0:ntok], lhsT=wv[:, k, mcol:mcol + 128],
                                                             rhs=rhsT[:, k, 0:ntok], start=(k == 0), stop=(k == NCH - 1)),
                         reads=[("wr", slot)] + rhs_cells, writes=[("ps", bk)])
                return bk

            def fm_pair(T, wv, rhsT, rhs_cells, slot):
                ntok = T["ntok"]
                b = alloc_bank()
                for m in range(2):
                    for k in range(NCH):
                        P.op("pe", lambda e, k=k, m=m: e.matmul(ps[:, b, m * 256:m * 256 + ntok], lhsT=wv[:, k, m * 128:(m + 1) * 128],
                                                               rhs=rhsT[:, k, 0:ntok], start=(k == 0), stop=(k == NCH - 1)),
                             reads=[("wr", slot)] + rhs_cells, writes=fb(b))
                return b, ps[:, b, :].rearrange("p (m c) -> p m c", m=2)[:, :, 0:ntok]

            def post_norm_A(T, pr, eps=EPS):
                PT = T["PT"]
                src = ps[0:PT, pr:pr + 2, :]
                ssc = stat()
                P.op("act", lambda e: e.activation(out=junk[0:PT, :].rearrange("p (a b) -> p a b", a=2), in_=src, func=AF.Square,
                                                   accum_out=stats[0:PT, ssc:ssc + 1]),
                     reads=fb(pr) + fb(pr + 1), writes=[("junk",), ("st", ssc)])
                return rstd_from(ssc, PT, 1.0 / D, eps)

            def post_norm_B(T, xb, s, pr, r, gbc, gcell):
                PT = T["PT"]
                src = ps[0:PT, pr:pr + 2, :]
                tk = rotn("tok", 2)
                P.op("dve", lambda e: e.scalar_tensor_tensor(out=tok[tk][0:PT, :].rearrange("p (a b) -> p a b", a=2), in0=src,
                                                             scalar=stats[0:PT, r:r + 1],
                                                             in1=gbc[0:PT, :].rearrange("p (a b) -> p a b", a=2),
                                                             op0=ALU.mult, op1=ALU.mult),
                     reads=fb(pr) + fb(pr + 1) + [("st", r), (gcell,)], writes=[("tok", tk)])
                P.op("pool", lambda e: e.tensor_tensor(out=xbuf[xb][0:PT, s, :], in0=xbuf[xb][0:PT, s, :], in1=tok[tk][0:PT, :], op=ALU.add),
                     reads=[("x", xb, s), ("tok", tk)], writes=[("x", xb, s)])
                release_pair(pr)

            def emit_states(T):
                samp = T["kind"] == "s"
                ntok = T["ntok"]
                for b in range(2 if samp else 1):
                    bb = b if samp else T["b"]
                    hdst, cdst = (hs_o, cs_o) if samp else (hp, cp)
                    prh = alloc_pair()
                    prc = alloc_pair()
                    for c in range(NCH):
                        P.op("pe", lambda e, c=c, b=b, prh=prh: e.transpose(out=ps[0:1, prh + c // 4, (c % 4) * 128:(c % 4 + 1) * 128],
                                                                           in_=hst[:, c, b:b + 1], identity=identf[:]),
                             reads=[("hst", c), ("identf",)], writes=fb(prh + c // 4))
                        c0 = (b * 35 + 32) if samp else ntok
                        P.op("pe", lambda e, c=c, c0=c0, prc=prc: e.transpose(out=ps[0:3, prc + c // 4, (c % 4) * 128:(c % 4 + 1) * 128],
                                                                             in_=xaT[:, c, c0:c0 + 3], identity=identf[:]),
                             reads=[("xam", c), ("identf",)], writes=fb(prc + c // 4))
                    sr = gv[b]
                    gvc_ = [("gv", b, q) for q in range(4)]
                    P.op("dve", lambda e, prh=prh, sr=sr: e.tensor_copy(out=sr[0:1, :].rearrange("p (a b) -> p a b", a=2), in_=ps[0:1, prh:prh + 2, :]),
                         reads=fb(prh) + fb(prh + 1), writes=gvc_)
                    P.op("sp", lambda e, bb=bb, hdst=hdst, sr=sr: e.dma_start(out=hdst[bb:bb + 1, :], in_=sr[0:1, :]),
                         reads=gvc_, dma_sem=so_sem, serial=True)
                    sr3 = tok[b]
                    P.op("dve", lambda e, prc=prc, sr3=sr3: e.tensor_copy(out=sr3[0:3, :].rearrange("p (a b) -> p a b", a=2), in_=ps[0:3, prc:prc + 2, :]),
                         reads=fb(prc) + fb(prc + 1), writes=[("tok", b)])
                    P.op("sp", lambda e, bb=bb, cdst=cdst, sr3=sr3: e.dma_start(out=cdst[bb, :, :], in_=sr3[0:3, :]),
                         reads=[("tok", b)], dma_sem=so_sem, serial=True)
                    release_pair(prh)
                    release_pair(prc)

            def mixer_stages(ti):
                T = tiles[ti]
                xb = ti % 3
                xnT = xnTs[ti % 2]
                ntok, nsub, PT = T["ntok"], T["nsub"], T["PT"]
                samp = T["kind"] == "s"
                xn_cells = [(("xnT", ti % 2), s) for s in range(nsub)]
                gcost = 8 * max(ntok, 64)
                st = []

                def segv(ap2d):
                    if samp:
                        return ap2d[:, 0:64].rearrange("p (b t) -> p b t", t=32)
                    return ap2d[:, 0:ntok]

                def xa_win(c, k):
                    if samp:
                        return xaT[:, c, 0:70].rearrange("p (b w) -> p b w", w=35)[:, :, k:k + 32]
                    return xaT[:, c, k:k + ntok]

                def s_init():
                    if T["first"]:
                        if samp:
                            for b in range(2):
                                cdma(lambda e, b=b: e.dma_start(out=hst[:, :, b], in_=sh_in[b].rearrange("(c p) -> p c", p=128),
                                                                allow_slow_non_contiguous=True),
                                     writes=[("hst", c) for c in range(NCH)])
                                for c in range(NCH):
                                    cdma(lambda e, b=b, c=c: e.dma_start(
                                        out=xaT[:, c, b * 35:b * 35 + 3],
                                        in_=sc_in[b, :, c * 128:(c + 1) * 128].rearrange("k p -> p k"),
                                        allow_slow_non_contiguous=True),
                                        writes=[("xah",)])
                        else:
                            P.op("pool", lambda e: e.memset(hst[:], 0.0), writes=[("hst", c) for c in range(NCH)])
                            P.op("pool", lambda e: e.memset(xaT[:, :, 0:3], 0.0), writes=[("xah",)])
                st.append((0, s_init))

                def s_loadx():
                    if ti + 1 < NT:
                        load_x(ti + 1)

                def s_xa(h):
                    slot, wv = ws_acquire(WIN[h])
                    b, pv = fm_pair(T, wv, xnT, xn_cells, slot)
                    if samp:
                        for m in range(2):
                            c = h * 2 + m
                            P.op("act", lambda e, c=c, m=m: e.activation(out=xa_win(c, 3), in_=segv(ps[:, b, m * 256:(m + 1) * 256]), func=AF.Copy),
                                 reads=fb(b), writes=[("xam", c)])
                    else:
                        P.op("act", lambda e: e.activation(out=xaT[:, 2 * h:2 * h + 2, 3:3 + ntok], in_=pv, func=AF.Copy),
                             reads=fb(b), writes=[("xam", 2 * h), ("xam", 2 * h + 1)])
                    ws_release()

                def s_conv(c):
                    P.op("dve", lambda e: e.tensor_scalar(out=segv(xcf[:, c, :]), in0=xa_win(c, 0), scalar1=cw[:, c, 0:1],
                                                          scalar2=cbT[:, c:c + 1], op0=ALU.mult, op1=ALU.add),
                         reads=[("xam", c), ("xah",), ("cw",), ("cbT",)], writes=[("xcf", c)])
                    for k in range(1, 4):
                        P.op("dve", lambda e, k=k: e.scalar_tensor_tensor(out=segv(xcf[:, c, :]), in0=xa_win(c, k),
                                                                          scalar=cw[:, c, k:k + 1], in1=segv(xcf[:, c, :]),
                                                                          op0=ALU.mult, op1=ALU.add),
                             reads=[("xam", c), ("xah",), ("cw",), ("xcf", c)], writes=[("xcf", c)])
                    if c >= 1:
                        s_xcb1(c - 1)
                    if c == NCH - 1 and not samp and not T["last"]:
                        P.op("dve", lambda e: e.tensor_copy(out=xaT[:, :, 0:3], in_=xaT[:, :, ntok:ntok + 3]),
                             reads=[("xam", cc) for cc in range(NCH)], writes=[("xah",)])

                def s_xcb1(c):
                    P.op("act", lambda e: e.activation(out=xcb[:, c, 0:ntok], in_=xcf[:, c, 0:ntok], func=AF.Copy),
                         reads=[("xcf", c)], writes=[("xcb", c)])

                def s_xcb():
                    s_xcb1(NCH - 1)

                def s_ga(h):
                    slot, wv = ws_acquire(WIN[4 + h])
                    b, pv = fm_pair(T, wv, xnT, xn_cells, slot)
                    P.op("act", lambda e: e.activation(out=gga[:, 2 * h:2 * h + 2, 0:ntok], in_=pv, func=AF.Gelu_apprx_tanh),
                         reads=fb(b), writes=[("gga", 2 * h), ("gga", 2 * h + 1)])
                    ws_release()

                def s_gate(c):
                    bg_ = alloc_bank()
                    P.op("pe", lambda e: e.matmul(ps[:, bg_, 0:ntok], lhsT=wab[:, c, :], rhs=xcb[:, c, 0:ntok], start=True, stop=True),
                         reads=[("wab",), ("xcb", c)], writes=fb(bg_))
                    P.op("pe", lambda e: e.matmul(ps[:, bg_, 256:256 + ntok], lhsT=wxb[:, c, :], rhs=xcb[:, c, 0:ntok], start=True, stop=True),
                         reads=[("wxb",), ("xcb", c)], writes=fb(bg_))
                    tr = rotn("chTr", 2)
                    sl = c % 4
                    Tr, Ti, Aa, Ee = chTr[tr], chTi[sl], chA[sl], chE[sl]
                    P.op("act", lambda e: e.activation(out=Tr[:, 0:ntok], in_=ps[:, bg_, 0:ntok], func=AF.Tanh,
                                                       scale=0.5, bias=ba2[:, c:c + 1]),
                         reads=fb(bg_) + [("ba2",)], writes=[("chTr", tr)])
                    P.op("act", lambda e: e.activation(out=Ti[:, 0:ntok], in_=ps[:, bg_, 256:256 + ntok], func=AF.Tanh,
                                                       scale=0.5, bias=bx2[:, c:c + 1]),
                         reads=fb(bg_) + [("bx2",)], writes=[("chTi", sl)])
                    P.op("act", lambda e: e.activation(out=Aa[:, 0:ntok], in_=Tr[:, 0:ntok], func=AF.Exp,
                                                       scale=hcn[:, c:c + 1], bias=hcn[:, c:c + 1]),
                         reads=[("chTr", tr), ("hcn",)], writes=[("chA", sl)])
                    P.op("act", lambda e: e.activation(out=Ee[:, 0:ntok], in_=Tr[:, 0:ntok], func=AF.Exp,
                                                       scale=cneg[:, c:c + 1], bias=cneg[:, c:c + 1]),
                         reads=[("chTr", tr), ("cneg",)], writes=[("chE", sl)])
                    P.op("dve", lambda e: e.scalar_tensor_tensor(out=Ti[:, 0:ntok], in0=Ti[:, 0:ntok], scalar=1.0,
                                                                 in1=xcf[:, c, 0:ntok], op0=ALU.add, op1=ALU.mult),
                         reads=[("chTi", sl), ("xcf", c)], writes=[("chTi", sl)])

                def s_sqrt(grp):
                    for c in range(grp * 4, grp * 4 + 4):
                        sl = c % 4
                        Ee = chE[sl]
                        P.op("act", lambda e, Ee=Ee: e.activation(out=Ee[:, 0:ntok], in_=Ee[:, 0:ntok], func=AF.Sqrt,
                                                                 scale=-0.25, bias=0.25),
                             reads=[("chE", sl)], writes=[("chE", sl)])

                def s_scan(c):
                    sl = c % 4
                    Ti, Aa, Ee = chTi[sl], chA[sl], chE[sl]
                    hh = rotn("chH", 2)
                    Hh = chH[hh]
                    P.op("dve", lambda e: e.tensor_tensor(out=Ti[:, 0:ntok], in0=Ee[:, 0:ntok], in1=Ti[:, 0:ntok], op=ALU.mult),
                         reads=[("chTi", sl), ("chE", sl)], writes=[("chTi", sl)])
                    nseg = 2 if samp else 1
                    sl_len = 32 if samp else ntok
                    for b in range(nseg):
                        c0 = b * sl_len
                        P.op("dve", lambda e, b=b, c0=c0: e.tensor_tensor_scan(
                            out=Hh[:, c0:c0 + sl_len], data0=Aa[:, c0:c0 + sl_len], data1=Ti[:, c0:c0 + sl_len],
                            initial=hst[:, c, b:b + 1], op0=ALU.mult, op1=ALU.add),
                            reads=[("chA", sl), ("chTi", sl), ("hst", c), ("chH", hh)], writes=[("chH", hh)])
                    if samp:
                        P.op("dve", lambda e: e.tensor_copy(
                            out=hst[:, c, 0:2], in_=Hh[:, 0:64].rearrange("p (b t) -> p b t", t=32)[:, :, 31]),
                            reads=[("chH", hh)], writes=[("hst", c)])
                    else:
                        P.op("dve", lambda e: e.tensor_copy(out=hst[:, c, 0:1], in_=Hh[:, ntok - 1:ntok]),
                             reads=[("chH", hh)], writes=[("hst", c)])
                    P.op("dve", lambda e: e.tensor_tensor(out=hgT[:, c, 0:ntok], in0=Hh[:, 0:ntok], in1=gga[:, c, 0:ntok], op=ALU.mult),
                         reads=[("chH", hh), ("gga", c)], writes=[("hgT", c)])

                def s_u(h):
                    slot, wv = ws_acquire(WIN[8 + h])
                    b, pv = fm_pair(T, wv, xnT, xn_cells, slot)
                    P.op("act", lambda e: e.activation(out=gu[:, 2 * h:2 * h + 2, 0:ntok], in_=pv, func=AF.Gelu_apprx_tanh),
                         reads=fb(b), writes=[("gu", 2 * h), ("gu", 2 * h + 1)])
                    ws_release()

                def s_v(h):
                    slot, wv = ws_acquire(WIN[12 + h])
                    for s in range(nsub):
                        gsl = s % 2
                        bk = alloc_half()
                        for k in range(NCH):
                            P.op("pe", lambda e, k=k, bk=bk, s=s: e.matmul(psh[0:PT, bk, 0:256], lhsT=xnT[:, k, s * 128:s * 128 + PT],
                                                                          rhs=wv[:, k, :], start=(k == 0), stop=(k == NCH - 1)),
                                 reads=[("wr", slot), (("xnT", ti % 2), s)], writes=[("ps", bk)])
                        P.op("act", lambda e, bk=bk, gsl=gsl: e.activation(out=gv[gsl][0:PT, h * 256:(h + 1) * 256],
                                                                           in_=psh[0:PT, bk, 0:256], func=AF.Gelu_apprx_tanh),
                             reads=[("ps", bk)], writes=[("gv", gsl, h)])
                        P.op("dve", lambda e, gsl=gsl: e.bn_stats(out=st6[gsl][0:PT, h, :], in_=gv[gsl][0:PT, h * 256:(h + 1) * 256]),
                             reads=[("gv", gsl, h)], writes=[("st6", gsl, h)])
                    ws_release()

                def s_ln():
                    rr = []
                    for s in range(nsub):
                        gsl = s % 2
                        P.op("dve", lambda e, gsl=gsl: e.bn_aggr(out=mvt[gsl][0:PT, :], in_=st6[gsl][0:PT].rearrange("p a b -> p (a b)")),
                             reads=[("st6", gsl, q) for q in range(4)], writes=[("mvt", gsl)])
                        vsc = stat()
                        P.op("pool", lambda e, gsl=gsl, vsc=vsc: e.tensor_copy(out=stats[0:PT, vsc:vsc + 1], in_=mvt[gsl][0:PT, 1:2]),
                             reads=[("mvt", gsl)], writes=[("st", vsc)])
                        rr.append(rstd_from(vsc, PT, 1.0, EPS))
                    for s in range(nsub):
                        gsl = s % 2
                        r = rr[s]
                        gvc = [("gv", gsl, q) for q in range(4)]
                        P.op("dve", lambda e, gsl=gsl, r=r: e.tensor_scalar(out=gv[gsl][0:PT, :], in0=gv[gsl][0:PT, :],
                                                                           scalar1=mvt[gsl][0:PT, 0:1], scalar2=stats[0:PT, r:r + 1],
                                                                           op0=ALU.subtract, op1=ALU.mult),
                             reads=gvc + [("mvt", gsl), ("st", r)], writes=gvc)
                        P.op("pool", lambda e, gsl=gsl: e.tensor_tensor(out=gv[gsl][0:PT, :], in0=gv[gsl][0:PT, :], in1=gbc_lng[0:PT, :],
                                                                       op=ALU.mult),
                             reads=gvc + [("gbc_lng",)], writes=gvc)

                def s_ln2():
                    for s in range(nsub):
                        gsl = s % 2
                        gvc = [("gv", gsl, q) for q in range(4)]
                        if samp:
                            P.op("dve", lambda e, gsl=gsl: e.tensor_tensor(out=gv[gsl][0:PT, :], in0=gv[gsl][0:PT, :], in1=gbc_lnb[0:PT, :],
                                                                          op=ALU.add),
                                 reads=gvc + [("gbc_lnb",)], writes=gvc)
                            P.op("act", lambda e, gsl=gsl: e.activation(out=vn[gsl][0:PT, :], in_=gv[gsl][0:PT, :], func=AF.Copy),
                                 reads=gvc, writes=[("vn", gsl)])
                            P.op("sp", lambda e, gsl=gsl: e.dma_start(out=vs_o.rearrange("b t d -> (b t) d"), in_=gv[gsl][0:PT, :]),
                                 reads=gvc, dma_sem=vs_sem)
                        else:
                            P.op("dve", lambda e, gsl=gsl: e.tensor_tensor(out=vn[gsl][0:PT, :], in0=gv[gsl][0:PT, :], in1=gbc_lnb[0:PT, :],
                                                                          op=ALU.add),
                                 reads=gvc + [("gbc_lnb",)], writes=[("vn", gsl)])

                prep_rs = []

                def s_states():
                    if T["last"]:
                        emit_states(T)

                def s_prepA():
                    if ti + 1 < NT:
                        prep_rs.extend(norm_A(tiles[ti + 1], (ti + 1) % 3))

                def s_prepB():
                    if ti + 1 < NT:
                        norm_B(tiles[ti + 1], (ti + 1) % 3, prep_rs)

                def s_prepC():
                    if ti + 1 < NT:
                        norm_C(tiles[ti + 1], gT_pm, "gT_pm", xnTs[(ti + 1) % 2], ("xnT", (ti + 1) % 2))

                def s_spatial(gp):
                    b = alloc_bank()
                    for gi in range(2):
                        g = 2 * gp + gi
                        if samp:
                            for bb in range(2):
                                wsb_ = wsS0 if bb == 0 else wsS1
                                P.op("pe", lambda e, bb=bb, wsb_=wsb_, g=g, gi=gi: e.matmul(
                                    ps[:, b, gi * 256 + bb * 32:gi * 256 + (bb + 1) * 32], lhsT=vn[0][0:64, g * 128:(g + 1) * 128],
                                    rhs=wsb_[:, g, :], start=True, stop=False),
                                    reads=[("vn", 0), ("wsS0",), ("wsS1",)], writes=fb(b))
                                P.op("pe", lambda e, bb=bb, g=g, gi=gi: e.matmul(
                                    ps[:, b, gi * 256 + bb * 32:gi * 256 + (bb + 1) * 32], lhsT=ones128[0:64, :],
                                    rhs=bs128[0:64, g, 0:32], start=False, stop=True),
                                    reads=[("ones128",), ("bs128",)], writes=fb(b))
                        else:
                            for s in range(nsub):
                                P.op("pe", lambda e, s=s, g=g, gi=gi: e.matmul(
                                    ps[:, b, gi * 256 + s * 128:gi * 256 + (s + 1) * 128], lhsT=vn[s % 2][:, g * 128:(g + 1) * 128],
                                    rhs=wsT[:, g, :], start=True, stop=False),
                                    reads=[("vn", s % 2), ("wsT",)], writes=fb(b))
                                P.op("pe", lambda e, s=s, g=g, gi=gi: e.matmul(
                                    ps[:, b, gi * 256 + s * 128:gi * 256 + (s + 1) * 128], lhsT=ones128[:, :],
                                    rhs=bs128[:, g, :], start=False, stop=True),
                                    reads=[("ones128",), ("bs128",)], writes=fb(b))
                    pv = ps[:, b, :].rearrange("p (m c) -> p m c", m=2)[:, :, 0:ntok]
                    P.op("dve", lambda e: e.tensor_tensor(out=gsT[:, 2 * gp:2 * gp + 2, 0:ntok], in0=gu[:, 2 * gp:2 * gp + 2, 0:ntok], in1=pv,
                                                          op=ALU.mult),
                         reads=[("gu", 2 * gp), ("gu", 2 * gp + 1)] + fb(b), writes=[("gsT", 2 * gp), ("gsT", 2 * gp + 1)])

                gs_cells = [("gsT", g) for g in range(8)]
                hg_cells = [("hgT", c) for c in range(NCH)]

                def s_gb(mh):
                    slot, wv = ws_acquire(WIN[20 + mh])
                    b, pv = fm_pair(T, wv, xnT, xn_cells, slot)
                    P.op("act", lambda e: e.activation(out=Tsig[:, :, 0:ntok], in_=pv, func=AF.Tanh, scale=0.5),
                         reads=fb(b), writes=[("Tsig", 0), ("Tsig", 1)])
                    ws_release()

                def s_ob(mh):
                    slot, wv = ws_acquire(BRB[mh])
                    b, pv = fm_pair(T, wv, gsT, gs_cells, slot)
                    P.op("dve", lambda e: e.scalar_tensor_tensor(out=t2[:, 2 * mh:2 * mh + 2, 0:ntok], in0=Tsig[:, :, 0:ntok], scalar=1.0,
                                                                 in1=pv, op0=ALU.add, op1=ALU.mult),
                         reads=[("Tsig", 0), ("Tsig", 1)] + fb(b), writes=[("t2", 2 * mh), ("t2", 2 * mh + 1)])
                    ws_release()

                def s_ga2(mh):
                    slot, wv = ws_acquire(WIN[16 + mh])
                    b, pv = fm_pair(T, wv, xnT, xn_cells, slot)
                    P.op("act", lambda e: e.activation(out=Tsig[:, :, 0:ntok], in_=pv, func=AF.Tanh, scale=0.5),
                         reads=fb(b), writes=[("Tsig", 0), ("Tsig", 1)])
                    ws_release()

                def s_oa(mh):
                    slot, wv = ws_acquire(BRA[mh])
                    b, pv = fm_pair(T, wv, hgT, hg_cells, slot)
                    P.op("dve", lambda e: e.scalar_tensor_tensor(out=tmpA2[:, :, 0:ntok], in0=Tsig[:, :, 0:ntok], scalar=1.0,
                                                                 in1=pv, op0=ALU.add, op1=ALU.mult),
                         reads=[("Tsig", 0), ("Tsig", 1)] + fb(b), writes=[("tmpA2",)])
                    P.op("pool", lambda e: e.tensor_tensor(out=mixT[:, 2 * mh:2 * mh + 2, 0:ntok], in0=tmpA2[:, :, 0:ntok],
                                                           in1=t2[:, 2 * mh:2 * mh + 2, 0:ntok], op=ALU.add),
                         reads=[("tmpA2",), ("t2", 2 * mh), ("t2", 2 * mh + 1)], writes=[("mixT", 2 * mh), ("mixT", 2 * mh + 1)])
                    ws_release()

                mix_cells = [("mixT", c) for c in range(NCH)]
                prs = []

                def s_out(q):
                    if q == 0:
                        for s in range(nsub):
                            prs.append(alloc_pair())
                    slot, wv = ws_acquire(WOUT[q])
                    for s in range(nsub):
                        bkq = prs[s] + q // 2
                        cq = (q % 2) * 256
                        for k in range(NCH):
                            P.op("pe", lambda e, k=k, s=s, bkq=bkq, cq=cq: e.matmul(
                                ps[0:PT, bkq, cq:cq + 256], lhsT=mixT[:, k, s * 128:s * 128 + PT], rhs=wv[:, k, :],
                                start=(k == 0), stop=(k == NCH - 1)),
                                reads=[("wr", slot)] + mix_cells, writes=[("ps", 2 * bkq + (q % 2))])
                    ws_release()

                pn_rs = []
                hn_rs = []

                def s_pnA():
                    for s in range(nsub):
                        pn_rs.append(post_norm_A(T, prs[s], eps=4.0 * EPS))

                def s_pnB():
                    for s in range(nsub):
                        post_norm_B(T, xb, s, prs[s], pn_rs[s], gbc_pm, "gbc_pm")
                    hn_rs.extend(norm_A(T, xb))

                def s_hnB():
                    norm_B(T, xb, hn_rs)

                US = 2400
                for h in range(4):
                    st.append((2 * gcost, lambda h=h: s_xa(h)))
                for h in range(4):
                    st.append((int(1.8 * US), lambda c=2 * h: s_conv(c)))
                    st.append((nsub * 8 * 256, lambda h=h: s_v(h)))
                    st.append((int(1.8 * US), lambda c=2 * h + 1: s_conv(c)))
                st.append((0, s_xcb))
                st.append((int(2.5 * US), s_ln))
                for h in range(4):
                    st.append((2 * gcost, lambda h=h: s_ga(h)))
                    if h == 1:
                        st.append((int(2.5 * US), s_ln2))
                for c in range(4):
                    st.append((int(1.8 * US), lambda c=c: s_gate(c)))
                    st.append((2 * gcost, lambda h=c: s_u(h)))
                st.append((int(2.5 * US), lambda: s_sqrt(0)))
                for g in range(2):
                    st.append((int(1.8 * US), lambda c=2 * g: s_scan(c)))
                    st.append((8 * ntok, lambda g=g: s_spatial(g)))
                    st.append((int(1.8 * US), lambda c=2 * g + 1: s_scan(c)))
                for c in range(4, 8):
                    st.append((int(1.8 * US), lambda c=c: s_gate(c)))
                    if c % 2 == 1:
                        st.append((8 * ntok, lambda g=(c - 1) // 2: s_spatial(g)))
                st.append((int(2.5 * US), lambda: s_sqrt(1)))
                st.append((0, s_loadx))
                st.append((int(2.0 * US), s_prepA))
                for mh in range(4):
                    st.append((2 * gcost, lambda mh=mh: s_gb(mh)))
                    st.append((int(1.8 * US), lambda c=4 + mh: s_scan(c)))
                    st.append((2 * gcost, lambda mh=mh: s_ob(mh)))
                    if mh == 0:
                        st.append((int(2.0 * US), s_prepB))
                    if mh == 2:
                        st.append((int(3.0 * US), s_prepC))
                st.append((0, s_states))
                for mh in range(4):
                    st.append((2 * gcost, lambda mh=mh: s_ga2(mh)))
                    st.append((2 * gcost, lambda mh=mh: s_oa(mh)))
                for q in range(4):
                    st.append((nsub * 8 * 256, lambda q=q: s_out(q)))
                st.append((int(2.0 * US), s_pnA))
                st.append((int(3.5 * US), s_pnB))
                st.append((int(2.0 * US), s_hnB))
                return st

            def ffn_stages(ti):
                T = tiles[ti]
                xb = ti % 3
                ntok, nsub, PT = T["ntok"], T["nsub"], T["PT"]
                samp = T["kind"] == "s"
                hn_cells = [("hnT", s) for s in range(nsub)]
                ff_cells = [("ffT", j) for j in range(NFF)]
                gcost = 8 * max(ntok, 64)
                st = []
                prs = []

                def s_fin(q):
                    slg, wvg = ws_acquire(FING[q])
                    slu, wvu = ws_acquire(FINU[q])
                    for jj in range(2):
                        j = 2 * q + jj
                        b = alloc_bank()
                        for m, (wv_, sl_) in enumerate(((wvg, slg), (wvu, slu))):
                            for k in range(NCH):
                                P.op("pe", lambda e, k=k, m=m, wv_=wv_, jj=jj, b=b: e.matmul(
                                    ps[:, b, m * 256:m * 256 + ntok], lhsT=wv_[:, k, jj * 128:(jj + 1) * 128],
                                    rhs=hnT[:, k, 0:ntok], start=(k == 0), stop=(k == NCH - 1)),
                                    reads=[("wr", sl_)] + hn_cells, writes=fb(b))
                        pv = ps[:, b, :].rearrange("p (m c) -> p m c", m=2)[:, :, 0:ntok]
                        sg = rotn("sgt", 2)
                        P.op("act", lambda e, pv=pv, sg=sg: e.activation(out=sgt[sg][:, 0:ntok], in_=pv[:, 0, :], func=AF.Tanh, scale=0.5),
                             reads=fb(b), writes=[("sgt", sg)])
                        P.op("dve", lambda e, pv=pv, sg=sg: e.scalar_tensor_tensor(out=sgt[sg][:, 0:ntok], in0=sgt[sg][:, 0:ntok], scalar=1.0,
                                                                               in1=pv[:, 0, :], op0=ALU.add, op1=ALU.mult),
                             reads=[("sgt", sg)] + fb(b), writes=[("sgt", sg)])
                        P.op("dve", lambda e, pv=pv, sg=sg, j=j: e.scalar_tensor_tensor(out=ffT[:, j, 0:ntok], in0=sgt[sg][:, 0:ntok], scalar=0.5,
                                                                                    in1=pv[:, 1, :], op0=ALU.mult, op1=ALU.mult),
                             reads=[("sgt", sg)] + fb(b), writes=[("ffT", j)])
                    ws_release()
                    ws_release()

                def s_fout(r):
                    if r == 0:
                        for s in range(nsub):
                            prs.append(alloc_pair())
                    slot, wv = ws_acquire(FOUT[r])
                    for s in range(nsub):
                        for half in range(2):
                            for kk in range(2):
                                j = r * 2 + kk
                                P.op("pe", lambda e, j=j, kk=kk, s=s, half=half: e.matmul(
                                    ps[0:PT, prs[s] + half, :], lhsT=ffT[:, j, s * 128:s * 128 + PT],
                                    rhs=wv[:, kk, half * 512:(half + 1) * 512], start=(j == 0), stop=(j == NFF - 1)),
                                    reads=[("wr", slot)] + ff_cells, writes=fb(prs[s] + half))
                    ws_release()

                fin_rs = []

                def s_hnC():
                    norm_C(T, gT_pf, "gT_pf", hnT, "hnT")

                def s_finalA():
                    for s in range(nsub):
                        fin_rs.append(post_norm_A(T, prs[s]))

                def s_final():
                    for s in range(nsub):
                        post_norm_B(T, xb, s, prs[s], fin_rs[s], gbc_pf, "gbc_pf")
                        if samp:
                            P.op("sp", lambda e: e.dma_start(out=ys.rearrange("b t d -> (b t) d"), in_=xbuf[xb][0:PT, 0, :]),
                                 reads=[("x", xb, 0)], dma_sem=y_sems[xb])
                        else:
                            r0 = T["tt"] * TT + s * 128
                            P.op("sp", lambda e, s=s, r0=r0: e.dma_start(out=yp[T["b"], r0:r0 + 128, :], in_=xbuf[xb][:, s, :]),
                                 reads=[("x", xb, s)], dma_sem=y_sems[xb])

                st.append((int(3.0 * 2400), s_hnC))
                for q in range(NFF // 2):
                    st.append((4 * gcost, lambda q=q: s_fin(q)))
                for r in range(NFF // 2):
                    st.append((int(FOUT_W * nsub * 4 * 512), lambda r=r: s_fout(r)))
                st.append((int(2.0 * 2400), s_finalA))
                st.append((int(3.5 * 2400), s_final))
                return st

            def timed(stages, t0):
                tot = sum(c for c, _ in stages) + 1e-9
                out = []
                acc = 0.0
                for c, f in stages:
                    out.append((t0 + acc / tot, f))
                    acc += c
                return out

            ws_prefetch()
            load_x(0)
            prep(0)
            allst = []
            for ti in range(NT):
                allst += [(t, 0, i, f) for i, (t, f) in enumerate(timed(mixer_stages(ti), float(ti)))]
                allst += [(t, 1, i, f) for i, (t, f) in enumerate(timed(ffn_stages(ti), ti + 1.0 + F_OFFSET))]
            if not INTERLEAVE:
                allst = [(float(int(t - (1.0 + F_OFFSET if k else 0.0)) + 0.5 * k), k, i, f) for (t, k, i, f) in allst]
            allst.sort(key=lambda z: (z[0], z[1], z[2]))
            for (_, _, _, f) in allst:
                f()

        record(_Dry(), True)
        P = Prog()
        record(P, False)
        P.emit(block, eng_sems, {"sp": [y_sems[0], y_sems[1], y_sems[2], so_sem, vs_sem]})
    return nc


_WNAMES = ["g_pre_mix", "w_in", "conv_w", "conv_b", "w_a", "b_a", "w_x", "b_x", "lam", "w_br_a", "ln_g", "ln_b",
           "w_s", "b_s", "w_br_b", "w_out", "g_post_mix", "g_pre_ffn", "w_ffn_in", "w_ffn_out", "g_post_ffn"]


def make_in_maps(inputs):
    f = lambda a: np.ascontiguousarray(np.asarray(a, dtype=np.float32))
    shared = {}
    for n in _WNAMES:
        a = f(inputs[n])[0]
        if n in ("b_a", "b_x"):
            a = a.reshape(-1)
        shared[n] = np.ascontiguousarray(a)
    xpr, xsm = f(inputs["x_prompt"]), f(inputs["x_sample"])
    sh, sc = f(inputs["state_rglru_h"])[0], f(inputs["state_rglru_conv"])[0]
    maps = []
    for i in range(NCORES):
        m = dict(shared)
        m["xp"] = np.ascontiguousarray(xpr[2 * i:2 * i + 2])
        m["xs"] = np.ascontiguousarray(xsm[2 * i:2 * i + 2])
        m["sh"] = np.ascontiguousarray(sh[2 * i:2 * i + 2])
        m["sc"] = np.ascontiguousarray(sc[2 * i:2 * i + 2])
        maps.append(m)
    return maps


def kernel(**inputs):
    nc = build_program()
    in_maps = make_in_maps(inputs)
    res = run_bass_kernel_spmd(nc, in_maps, core_ids=list(range(NCORES)))
    R = res.results
    cat = lambda k: np.concatenate([np.asarray(r[k], dtype=np.float32) for r in R], axis=0)
    y_prompt = cat("yp")
    y_sample = cat("ys")
    new_h_prompt = cat("hp")[None]
    new_conv_prompt = cat("cp")[None]
    new_h_sample = cat("hs")[None]
    new_conv_sample = cat("cs")[None]
    new_v_sample = cat("vs")[None]
    return (y_prompt, y_sample, new_h_prompt, new_conv_prompt, new_h_sample, new_conv_sample, new_v_sample)
```

```python
import contextlib
import numpy as np
import concourse.bass as bass
import concourse.mybir as mybir
from concourse.bass_utils import run_bass_kernel_spmd

F32 = mybir.dt.float32
BF16 = mybir.dt.bfloat16
AF = mybir.ActivationFunctionType
ALU = mybir.AluOpType

NCORES = 8
D = 1024
NCH = 8
DFF = 2816
NFF = 22
SEQ = 2048
DEC = 32
EPS = 1e-6
TT = 256
NSLOT = 8
CV_AHEAD = 24
INTERLEAVE = True
F_OFFSET = 0.35
FOUT_W = 1.0


class _Op:
    __slots__ = ("eng", "fn", "deps", "needed", "tok", "dma_sem")

    def __init__(self, eng, fn, dma_sem):
        self.eng = eng
        self.fn = fn
        self.deps = None
        self.needed = False
        self.tok = None
        self.dma_sem = dma_sem


class _Cell:
    __slots__ = ("w", "r")

    def __init__(self):
        self.w = None
        self.r = []


class Prog:
    ENGS = ("pe", "act", "dve", "pool", "sp")

    def __init__(self):
        self.ops = {e: [] for e in self.ENGS}
        self.cells = {}
        self.last_serial = {}

    def op(self, eng, fn, reads=(), writes=(), dma_sem=None, serial=False):
        o = _Op(eng, fn, dma_sem)
        deps = {}
        cells = self.cells
        if serial:
            prev = self.last_serial.get(id(dma_sem))
            if prev is not None:
                deps[id(prev)] = prev
            self.last_serial[id(dma_sem)] = o
        for k in reads:
            c = cells.get(k)
            if c is None:
                c = cells[k] = _Cell()
            if c.w is not None:
                deps[id(c.w)] = c.w
        for k in writes:
            c = cells.get(k)
            if c is None:
                c = cells[k] = _Cell()
            if c.w is not None:
                deps[id(c.w)] = c.w
            for r in c.r:
                deps[id(r)] = r
        for k in reads:
            cells[k].r.append(o)
        for k in writes:
            c = cells[k]
            c.w = o
            c.r = []
        deps.pop(id(o), None)
        dl = []
        for d in deps.values():
            if d.eng == "pe" and eng == "pe" and d.dma_sem is None and dma_sem is None:
                continue
            d.needed = True
            dl.append(d)
        o.deps = dl
        self.ops[eng].append(o)
        return o

    def emit(self, block, eng_sems, final_waits):
        cnt = {e: 0 for e in self.ENGS}
        dcnt = {}
        for e in self.ENGS:
            for o in self.ops[e]:
                if o.dma_sem is not None:
                    k = id(o.dma_sem)
                    dcnt[k] = dcnt.get(k, 0) + 16
                    o.tok = (o.dma_sem, dcnt[k])
                elif o.needed:
                    cnt[e] += 1
                    o.tok = (eng_sems[e], cnt[e])
        self.final_dma = dcnt

        def run(e, handle):
            waited = {}
            for o in self.ops[e]:
                need = {}
                for d in o.deps:
                    s, v = d.tok
                    k = id(s)
                    if need.get(k, (None, 0))[1] < v:
                        need[k] = (s, v)
                for k, (s, v) in need.items():
                    if waited.get(k, 0) < v:
                        handle.wait_ge(s, v)
                        waited[k] = v
                ins = o.fn(handle)
                if o.dma_sem is not None:
                    ins.then_inc(o.dma_sem, 16)
                elif o.needed:
                    ins.then_inc(eng_sems[e], 1)
            if e in final_waits:
                for s in final_waits[e]:
                    v = dcnt.get(id(s), 0)
                    if v:
                        handle.wait_ge(s, v)

        block.tensor(lambda h: run("pe", h))
        block.scalar(lambda h: run("act", h))
        block.vector(lambda h: run("dve", h))
        block.gpsimd(lambda h: run("pool", h))
        block.sync(lambda h: run("sp", h))


def build_program(n_prompt_tiles=SEQ // TT, do_sample=True, sample_first=False):
    nc = bass.Bass("TRN2", target_bir_lowering=False)

    def din(name, shape):
        return nc.dram_tensor(name, list(shape), F32, kind="ExternalInput").ap()

    def dout(name, shape):
        return nc.dram_tensor(name, list(shape), F32, kind="ExternalOutput").ap()

    xp = din("xp", [2, SEQ, D])
    xs_in = din("xs", [2, DEC, D])
    sh_in = din("sh", [2, D])
    sc_in = din("sc", [2, 3, D])
    g_pre_mix = din("g_pre_mix", [D])
    w_in = din("w_in", [D, 6 * D])
    conv_w = din("conv_w", [4, D])
    conv_b = din("conv_b", [D])
    w_a = din("w_a", [16, 64, 64])
    b_a = din("b_a", [D])
    w_x = din("w_x", [16, 64, 64])
    b_x = din("b_x", [D])
    lam = din("lam", [D])
    w_br_a = din("w_br_a", [D, D])
    ln_g = din("ln_g", [D])
    ln_b = din("ln_b", [D])
    w_s = din("w_s", [8, 128, 128])
    b_s = din("b_s", [8, 128])
    w_br_b = din("w_br_b", [D, D])
    w_out = din("w_out", [D, D])
    g_post_mix = din("g_post_mix", [D])
    g_pre_ffn = din("g_pre_ffn", [D])
    w_ffn_in = din("w_ffn_in", [D, 2 * DFF])
    w_ffn_out = din("w_ffn_out", [DFF, D])
    g_post_ffn = din("g_post_ffn", [D])

    yp = dout("yp", [2, SEQ, D])
    ys = dout("ys", [2, DEC, D])
    hp = dout("hp", [2, D])
    cp = dout("cp", [2, 3, D])
    hs_o = dout("hs", [2, D])
    cs_o = dout("cs", [2, 3, D])
    vs_o = dout("vs", [2, DEC, D])

    pieces = []

    def add_piece(kc, ncols, srcs):
        pieces.append(dict(kc=kc, ncols=ncols, srcs=srcs))
        return len(pieces) - 1

    def colblk(w, c0, n):
        return w[:, c0:c0 + n].rearrange("(k p) n -> p k n", p=128)

    WIN = [add_piece(8, 256, [(colblk(w_in, cb * 256, 256), 0, 256)]) for cb in range(24)]
    BRA = [add_piece(8, 256, [(colblk(w_br_a, h * 256, 256), 0, 256)]) for h in range(4)]
    BRB = [add_piece(8, 256, [(colblk(w_br_b, h * 256, 256), 0, 256)]) for h in range(4)]
    WOUT = [add_piece(8, 256, [(colblk(w_out, h * 256, 256), 0, 256)]) for h in range(4)]
    FING = [add_piece(8, 256, [(colblk(w_ffn_in, q * 256, 256), 0, 256)]) for q in range(NFF // 2)]
    FINU = [add_piece(8, 256, [(colblk(w_ffn_in, DFF + q * 256, 256), 0, 256)]) for q in range(NFF // 2)]
    FOUT = [add_piece(2, 1024, [(w_ffn_out[r * 256:(r + 1) * 256, :].rearrange("(k p) n -> p k n", p=128), 0, 1024)])
            for r in range(NFF // 2)]
    NPIECE = len(pieces)
    PSZ = 2048
    wsc = nc.dram_tensor("wsc", [NPIECE, 128, PSZ], BF16).ap()

    tiles = []
    for b in range(2):
        for tt in range(n_prompt_tiles):
            tiles.append(dict(kind="p", b=b, tt=tt, ntok=TT, nsub=TT // 128, PT=128,
                              first=(tt == 0), last=(tt == n_prompt_tiles - 1)))
    if do_sample:
        st = dict(kind="s", ntok=64, nsub=1, PT=64, first=True, last=True)
        if sample_first:
            tiles.insert(0, st)
        else:
            tiles.append(st)
    NT = len(tiles)

    with contextlib.ExitStack() as es:
        def sb(name, shape, dt=F32):
            return es.enter_context(nc.sbuf_tensor(name, list(shape), dt))

        def sem(name):
            return es.enter_context(nc.semaphore(name))

        wring = sb("wring", [128, NSLOT, PSZ], BF16)
        xbuf = [sb(f"xbuf{i}", [128, 2, D]) for i in range(3)]
        xsb = [sb(f"xsb{i}", [128, D], BF16) for i in range(2)]
        junk = sb("junk", [128, D], BF16)
        xnTs = [sb(f"xnT{i}", [128, NCH, TT], BF16) for i in range(2)]
        hnT = sb("hnT", [128, NCH, TT], BF16)
        tok = [sb(f"tok{i}", [128, D]) for i in range(2)]
        XAW = TT + 4
        xaT = sb("xaT", [128, NCH, XAW])
        xcf = sb("xcf", [128, NCH, TT])
        xcb = sb("xcb", [128, NCH, TT], BF16)
        chTr = [sb(f"chTr{i}", [128, TT]) for i in range(2)]
        chTi = [sb(f"chTi{i}", [128, TT]) for i in range(4)]
        chA = [sb(f"chA{i}", [128, TT]) for i in range(4)]
        chE = [sb(f"chE{i}", [128, TT]) for i in range(4)]
        chH = [sb(f"chH{i}", [128, TT]) for i in range(2)]
        gga = sb("gga", [128, NCH, TT])
        hgT = sb("hgT", [128, NCH, TT], BF16)
        gu = sb("gu", [128, NCH, TT])
        gsT = sb("gsT", [128, NCH, TT], BF16)
        gv = [sb(f"gv{i}", [128, D]) for i in range(2)]
        vn = [sb(f"vn{i}", [128, D], BF16) for i in range(2)]
        Tsig = sb("Tsig", [128, 2, TT])
        t2 = sb("t2", [128, NCH, TT], BF16)
        tmpA2 = sb("tmpA2", [128, 2, TT])
        mixT = sb("mixT", [128, NCH, TT], BF16)
        sgt = [sb(f"sgt{i}", [128, TT]) for i in range(2)]
        ffT = sb("ffT", [128, NFF, TT], BF16)
        hst = sb("hst", [128, NCH, 2])
        stats = sb("stats", [128, 96])
        st6 = [sb(f"st6_{i}", [128, 4, 6]) for i in range(2)]
        mvt = [sb(f"mvt{i}", [128, 2]) for i in range(2)]
        identf = sb("identf", [128, 128])
        identb = sb("identb", [128, 128], BF16)
        nhalf = sb("nhalf", [128, 1])
        ones128 = sb("ones128", [128, 128], BF16)
        gT_pm = sb("gT_pm", [128, NCH])
        gT_pf = sb("gT_pf", [128, NCH])
        cw = sb("cw", [128, NCH, 4])
        cbT = sb("cbT", [128, NCH])
        ba2 = sb("ba2", [128, NCH])
        bx2 = sb("bx2", [128, NCH])
        lamT = sb("lamT", [128, NCH])
        cneg = sb("cneg", [128, NCH])
        hcn = sb("hcn", [128, NCH])
        wab = sb("wab", [128, NCH, 128], BF16)
        wxb = sb("wxb", [128, NCH, 128], BF16)
        wsT = sb("wsT", [128, 8, 128], BF16)
        wsS0 = sb("wsS0", [64, 8, 32], BF16)
        wsS1 = sb("wsS1", [64, 8, 32], BF16)
        bs128 = sb("bs128", [128, 8, 128], BF16)
        gbc_lng = sb("gbc_lng", [128, D])
        gbc_lnb = sb("gbc_lnb", [128, D])
        gbc_pm = sb("gbc_pm", [128, D])
        gbc_pf = sb("gbc_pf", [128, D])

        ps = es.enter_context(nc.psum_tensor("ps", [128, 8, 512], F32))
        psb = ps[:].bitcast(BF16)
        psh = ps[:].rearrange("p b (h c) -> p (b h) c", h=2)

        eng_sems = {e: sem("s_" + e) for e in Prog.ENGS}
        cvt_sems = [sem(f"cv{i}") for i in range(16)]
        ring_sems = [sem(f"rg{i}") for i in range(NSLOT)]
        xl_sems = [sem(f"xl{i}") for i in range(3)]
        y_sems = [sem(f"yo{i}") for i in range(3)]
        c_sems = [sem(f"cst{i}") for i in range(8)]
        so_sem = sem("sto")
        vs_sem = sem("vso")
        block = es.enter_context(nc.Block())

        class _Dry:
            def op(self, *a, **k):
                return None

        stream_log = []

        def record(P, dry):
            stream = stream_log
            c_ctr = [0]

            def cdma(fn, reads=(), writes=()):
                i = c_ctr[0] % len(c_sems)
                c_ctr[0] += 1
                return P.op("sp", fn, reads=reads, writes=writes, dma_sem=c_sems[i], serial=True)

            held = set()
            lru = list(range(8))

            def fb(b):
                return [("ps", 2 * b), ("ps", 2 * b + 1)]

            def _touch(b):
                lru.remove(b)
                lru.append(b)

            def alloc_bank():
                for b in lru:
                    if b not in held:
                        _touch(b)
                        return b
                raise RuntimeError("no free PSUM bank")

            def alloc_half():
                return 2 * alloc_bank()

            def alloc_pair():
                best = None
                for b in range(0, 8, 2):
                    if b in held or (b + 1) in held:
                        continue
                    age = max(lru.index(b), lru.index(b + 1))
                    if best is None or age < best[0]:
                        best = (age, b)
                if best is None:
                    raise RuntimeError("no free PSUM bank pair")
                b = best[1]
                held.add(b)
                held.add(b + 1)
                _touch(b)
                _touch(b + 1)
                return b

            def release_pair(b):
                held.discard(b)
                held.discard(b + 1)
                _touch(b)
                _touch(b + 1)

            stat_ctr = [0]

            def stat():
                i = stat_ctr[0] % 96
                stat_ctr[0] += 1
                return i

            rot = {}

            def rotn(name, n):
                i = rot.get(name, 0)
                rot[name] = i + 1
                return i % n

            ws_state = dict(pos=0, loaded=0, cvt=0, held=0)
            cvt_done = set()

            def piece_view(ap2d, pc):
                return ap2d[:, 0:PSZ].rearrange("p (k n) -> p k n", k=pc["kc"])

            def rec_cvt(pid):
                pc = pieces[pid]
                dst = piece_view(wsc[pid], pc)
                for (src, c0, n) in pc["srcs"]:
                    P.op("pool", lambda e, dst=dst, src=src, c0=c0, n=n: e.dma_start(out=dst[:, :, c0:c0 + n], in_=src),
                         writes=[("wsc", pid)], dma_sem=cvt_sems[pid % 16], serial=True)

            def ws_prefetch():
                if dry:
                    return
                while ws_state["loaded"] < min(len(stream), ws_state["pos"] + NSLOT):
                    i = ws_state["loaded"]
                    j = ws_state["cvt"]
                    while j < min(len(stream), i + 1 + CV_AHEAD) and len(cvt_done) < NPIECE:
                        if stream[j] not in cvt_done:
                            cvt_done.add(stream[j])
                            rec_cvt(stream[j])
                        j += 1
                    ws_state["cvt"] = j
                    pid = stream[i]
                    slot = i % NSLOT
                    P.op("sp", lambda e, slot=slot, pid=pid: e.dma_start(out=wring[:, slot, :], in_=wsc[pid, :, :]),
                         reads=[("wsc", pid)], writes=[("wr", slot)], dma_sem=ring_sems[slot])
                    ws_state["loaded"] += 1

            def ws_acquire(pid):
                if dry:
                    stream.append(pid)
                    i = len(stream) - 1
                else:
                    i = ws_state["pos"] + ws_state["held"]
                    assert stream[i] == pid, (i, stream[i], pid)
                    assert i < ws_state["loaded"], (i, ws_state)
                ws_state["held"] += 1
                slot = i % NSLOT
                return slot, piece_view(wring[:, slot, :], pieces[pid])

            def ws_release():
                ws_state["held"] -= 1
                ws_state["pos"] += 1
                ws_prefetch()

            def rstd_from(ss_col, PT, scale, eps):
                t = stat()
                r = stat()
                P.op("pool", lambda e: e.tensor_scalar(out=stats[0:PT, t:t + 1], in0=stats[0:PT, ss_col:ss_col + 1],
                                                       scalar1=scale, scalar2=eps, op0=ALU.mult, op1=ALU.add),
                     reads=[("st", ss_col)], writes=[("st", t)])
                P.op("pool", lambda e: e.tensor_tensor(out=stats[0:PT, r:r + 1], in0=stats[0:PT, t:t + 1],
                                                       in1=nhalf[0:PT, :], op=ALU.pow),
                     reads=[("st", t), ("nhalf",)], writes=[("st", r)])
                return r

            def small_T(dst, src_vec, cellname):
                cdma(lambda e: e.dma_start(out=dst[:], in_=src_vec.rearrange("(c p) -> p c", p=128),
                                           allow_slow_non_contiguous=True),
                     writes=[(cellname,)])

            small_T(gT_pm, g_pre_mix, "gT_pm")
            small_T(gT_pf, g_pre_ffn, "gT_pf")
            small_T(cbT, conv_b, "cbT")
            small_T(ba2, b_a, "ba2")
            small_T(bx2, b_x, "bx2")
            small_T(lamT, lam, "lamT")
            for k in range(4):
                cdma(lambda e, k=k: e.dma_start(out=cw[:, :, k], in_=conv_w[k].rearrange("(c p) -> p c", p=128),
                                                allow_slow_non_contiguous=True),
                     writes=[("cw",)])
            for (dst, src, cn) in ((gbc_lng, ln_g, "gbc_lng"), (gbc_lnb, ln_b, "gbc_lnb"),
                                   (gbc_pm, g_post_mix, "gbc_pm"), (gbc_pf, g_post_ffn, "gbc_pf")):
                cdma(lambda e, dst=dst, src=src: e.dma_start(out=dst[:], in_=src.partition_broadcast(128)),
                     writes=[(cn,)])
            P.op("pool", lambda e: e.memset(identf[:], 0.0), writes=[("identf",)])
            P.op("pool", lambda e: e.affine_select(out=identf[:], in_=identf[:], compare_op=ALU.not_equal, fill=1.0,
                                                   base=0, pattern=[[-1, 128]], channel_multiplier=1),
                 reads=[("identf",)], writes=[("identf",)])
            P.op("pool", lambda e: e.tensor_copy(out=identb[:], in_=identf[:]), reads=[("identf",)], writes=[("identb",)])
            P.op("pool", lambda e: e.memset(nhalf[:], -0.5), writes=[("nhalf",)])
            P.op("pool", lambda e: e.memset(ones128[:], 1.0), writes=[("ones128",)])
            P.op("dve", lambda e: e.tensor_scalar(out=ba2[:], in0=ba2[:], scalar1=0.5, scalar2=None, op0=ALU.mult),
                 reads=[("ba2",)], writes=[("ba2",)])
            P.op("dve", lambda e: e.tensor_scalar(out=bx2[:], in0=bx2[:], scalar1=0.5, scalar2=None, op0=ALU.mult),
                 reads=[("bx2",)], writes=[("bx2",)])
            P.op("act", lambda e: e.activation(out=cneg[:], in_=lamT[:], func=AF.Abs),
                 reads=[("lamT",)], writes=[("cneg",)])
            P.op("act", lambda e: e.activation(out=cneg[:], in_=cneg[:], func=AF.Exp, scale=-1.0),
                 reads=[("cneg",)], writes=[("cneg",)])
            P.op("act", lambda e: e.activation(out=cneg[:], in_=cneg[:], func=AF.Ln, bias=1.0),
                 reads=[("cneg",)], writes=[("cneg",)])
            P.op("dve", lambda e: e.tensor_scalar(out=hcn[:], in0=lamT[:], scalar1=-1.0, scalar2=0.0, op0=ALU.mult, op1=ALU.max),
                 reads=[("lamT",)], writes=[("hcn",)])
            P.op("dve", lambda e: e.tensor_tensor(out=cneg[:], in0=cneg[:], in1=hcn[:], op=ALU.add),
                 reads=[("cneg",), ("hcn",)], writes=[("cneg",)])
            P.op("dve", lambda e: e.tensor_scalar(out=cneg[:], in0=cneg[:], scalar1=-8.0, scalar2=None, op0=ALU.mult),
                 reads=[("cneg",)], writes=[("cneg",)])
            P.op("dve", lambda e: e.tensor_scalar(out=hcn[:], in0=cneg[:], scalar1=0.5, scalar2=None, op0=ALU.mult),
                 reads=[("cneg",)], writes=[("hcn",)])
            wstage = gv[0][:].rearrange("p (c j) -> p c j", c=NCH)
            WST = [("gv", 0, q) for q in range(4)]
            for (wsrc, wdst, cn) in ((w_a, wab, "wab"), (w_x, wxb, "wxb")):
                P.op("pool", lambda e: e.memset(gv[0][:], 0.0), writes=WST)
                wv_ = wsrc.rearrange("(c two) i j -> two i c j", two=2)
                for two in range(2):
                    cdma(lambda e, two=two, wv_=wv_: e.dma_start(
                        out=wstage[two * 64:(two + 1) * 64, :, two * 64:(two + 1) * 64], in_=wv_[two]),
                        reads=WST, writes=[("wstage_dma", two)])
                P.op("dve", lambda e, wdst=wdst: e.tensor_copy(out=wdst[:], in_=wstage),
                     reads=WST + [("wstage_dma", 0), ("wstage_dma", 1)], writes=[(cn,)])
            cdma(lambda e: e.dma_start(out=wstage, in_=w_s.rearrange("g p q -> p g q")), writes=WST)
            for half in range(2):
                pr = alloc_bank()
                for gg in range(4):
                    g = half * 4 + gg
                    P.op("pe", lambda e, g=g, gg=gg, pr=pr: e.transpose(out=ps[:, pr, gg * 128:(gg + 1) * 128],
                                                                       in_=wstage[:, g, :], identity=identf[:]),
                         reads=WST + [("identf",)], writes=fb(pr))
                P.op("dve", lambda e, half=half, pr=pr: e.tensor_copy(
                    out=wsT[:, half * 4:(half + 1) * 4, :], in_=ps[:, pr, :].rearrange("p (g q) -> p g q", g=4)),
                    reads=fb(pr), writes=[("wsT",)])
            P.op("dve", lambda e: e.memset(wsT[64:128, :, 0:64], 0.0), reads=[("wsT",)], writes=[("wsT",)])
            P.op("dve", lambda e: e.memset(wsS0[:], 0.0), writes=[("wsS0",)])
            P.op("dve", lambda e: e.memset(wsS1[:], 0.0), writes=[("wsS1",)])
            P.op("dve", lambda e: e.tensor_copy(out=wsS0[0:32, :, :], in_=wsT[0:32, :, 0:32]), reads=[("wsT",), ("wsS0",)], writes=[("wsS0",)])
            cdma(lambda e: e.dma_start(out=wsS1[32:64, :, :], in_=wsS0[0:32, :, :]), reads=[("wsS0",), ("wsS1",)], writes=[("wsS1",)])
            bsv = b_s.rearrange("(o g) p -> o (g p)", o=1)
            bsf, bstf, bstb = tok[0], tok[1], junk
            for p0 in (0, 32):
                cdma(lambda e, p0=p0: e.dma_start(out=bsf[p0:p0 + 1, :], in_=bsv), writes=[("tok", 0)])
            bs33f = bs128[:].rearrange("p g q -> p (g q)")
            P.op("dve", lambda e: e.memset(bs128[:], 0.0), writes=[("bs128",)])
            P.op("dve", lambda e: e.tensor_copy(out=bs33f[0:1, :], in_=bsf[0:1, :]), reads=[("tok", 0), ("bs128",)], writes=[("bs128",)])
            P.op("dve", lambda e: e.tensor_copy(out=bstb[32:33, :], in_=bsf[32:33, :]), reads=[("tok", 0)], writes=[("junk",)])
            P.op("dve", lambda e: e.tensor_copy(out=bstf[32:33, :], in_=bstb[32:33, :]), reads=[("junk",)], writes=[("tok", 1)])
            P.op("dve", lambda e: e.tensor_tensor(out=bs33f[32:33, :], in0=bsf[32:33, :], in1=bstf[32:33, :], op=ALU.subtract),
                 reads=[("tok", 0), ("tok", 1), ("bs128",)], writes=[("bs128",)])

            def x_src(T):
                if T["kind"] == "p":
                    return xp[T["b"], T["tt"] * TT:(T["tt"] + 1) * TT, :].rearrange("(s p) d -> p s d", p=128)
                return xs_in.rearrange("b t d -> (b t) d")

            def load_x(ti):
                T = tiles[ti]
                xb = ti % 3
                PT, nsub = T["PT"], T["nsub"]
                if T["kind"] == "p":
                    P.op("sp", lambda e: e.dma_start(out=xbuf[xb][:, 0:nsub, :], in_=x_src(T)),
                         writes=[("x", xb, s) for s in range(nsub)], dma_sem=xl_sems[xb])
                else:
                    P.op("sp", lambda e: e.dma_start(out=xbuf[xb][0:PT, 0, :], in_=x_src(T)),
                         writes=[("x", xb, 0), ("x", xb, 1)], dma_sem=xl_sems[xb])

            def norm_A(T, xb):
                PT, nsub = T["PT"], T["nsub"]
                rs = []
                for s in range(nsub):
                    src = xbuf[xb][0:PT, s, :]
                    ssc = stat()
                    P.op("act", lambda e, src=src, ssc=ssc: e.activation(out=junk[0:PT, :], in_=src, func=AF.Square,
                                                                         accum_out=stats[0:PT, ssc:ssc + 1]),
                         reads=[("x", xb, s)], writes=[("junk",), ("st", ssc)])
                    rs.append(rstd_from(ssc, PT, 1.0 / D, EPS))
                return rs

            def norm_B(T, xb, rs):
                PT, nsub = T["PT"], T["nsub"]
                for s in range(nsub):
                    src = xbuf[xb][0:PT, s, :]
                    r = rs[s]
                    P.op("act", lambda e, src=src, r=r, s=s: e.activation(out=xsb[s][0:PT, :], in_=src, func=AF.Copy,
                                                                          scale=stats[0:PT, r:r + 1]),
                         reads=[("x", xb, s), ("st", r)], writes=[("xsb", s)])

            def norm_C(T, gT, gcell, dstT, dstname):
                PT, nsub = T["PT"], T["nsub"]
                for s in range(nsub):
                    bk = alloc_bank()
                    for c in range(NCH):
                        P.op("pe", lambda e, c=c, s=s, bk=bk: e.transpose(out=psb[:, bk, c * 128:c * 128 + PT],
                                                                         in_=xsb[s][0:PT, c * 128:(c + 1) * 128],
                                                                         identity=identb[0:PT, 0:PT]),
                             reads=[("xsb", s), ("identb",)], writes=fb(bk))
                    P.op("dve", lambda e, s=s, bk=bk: e.tensor_tensor(
                        out=dstT[:, :, s * 128:s * 128 + PT],
                        in0=psb[:, bk, :].rearrange("p (c t) -> p c t", c=NCH)[:, :, 0:PT],
                        in1=gT[:].unsqueeze(2).to_broadcast([128, NCH, PT]), op=ALU.mult),
                        reads=fb(bk) + [(gcell,)], writes=[(dstname, s)])

            def prep(ti):
                T = tiles[ti]
                xi = ti % 2
                rs = norm_A(T, ti % 3)
                norm_B(T, ti % 3, rs)
                norm_C(T, gT_pm, "gT_pm", xnTs[xi], ("xnT", xi))

            def fm_group(T, wv, mcol, rhsT, rhs_cells, slot):
                ntok = T["ntok"]
                bk = alloc_half()
                for k in range(NCH):
                    P.op("pe", lambda e, k=k, bk=bk: e.matmul(psh[:, bk, 0:ntok], lhsT=wv[:, k, mcol:mcol + 128],
                                                             rhs=rhsT[:, k, 0:ntok], start=(k == 0), stop=(k == NCH - 1)),
                         reads=[("wr", slot)] + rhs_cells, writes=[("ps", bk)])
                return bk

            def fm_pair(T, wv, rhsT, rhs_cells, slot):
                ntok = T["ntok"]
                b = alloc_bank()
                for m in range(2):
                    for k in range(NCH):
                        P.op("pe", lambda e, k=k, m=m: e.matmul(ps[:, b, m * 256:m * 256 + ntok], lhsT=wv[:, k, m * 128:(m + 1) * 128],
                                                               rhs=rhsT[:, k, 0:ntok], start=(k == 0), stop=(k == NCH - 1)),
                             reads=[("wr", slot)] + rhs_cells, writes=fb(b))
                return b, ps[:, b, :].rearrange("p (m c) -> p m c", m=2)[:, :, 0:ntok]

            def post_norm_A(T, pr, eps=EPS):
                PT = T["PT"]
                src = ps[0:PT, pr:pr + 2, :]
                ssc = stat()
                P.op("act", lambda e: e.activation(out=junk[0:PT, :].rearrange("p (a b) -> p a b", a=2), in_=src, func=AF.Square,
                                                   accum_out=stats[0:PT, ssc:ssc + 1]),
                     reads=fb(pr) + fb(pr + 1), writes=[("junk",), ("st", ssc)])
                return rstd_from(ssc, PT, 1.0 / D, eps)

            def post_norm_B(T, xb, s, pr, r, gbc, gcell):
                PT = T["PT"]
                src = ps[0:PT, pr:pr + 2, :]
                tk = rotn("tok", 2)
                P.op("dve", lambda e: e.scalar_tensor_tensor(out=tok[tk][0:PT, :].rearrange("p (a b) -> p a b", a=2), in0=src,
                                                             scalar=stats[0:PT, r:r + 1],
                                                             in1=gbc[0:PT, :].rearrange("p (a b) -> p a b", a=2),
                                                             op0=ALU.mult, op1=ALU.mult),
                     reads=fb(pr) + fb(pr + 1) + [("st", r), (gcell,)], writes=[("tok", tk)])
                P.op("pool", lambda e: e.tensor_tensor(out=xbuf[xb][0:PT, s, :], in0=xbuf[xb][0:PT, s, :], in1=tok[tk][0:PT, :], op=ALU.add),
                     reads=[("x", xb, s), ("tok", tk)], writes=[("x", xb, s)])
                release_pair(pr)

            def emit_states(T):
                samp = T["kind"] == "s"
                ntok = T["ntok"]
                for b in range(2 if samp else 1):
                    bb = b if samp else T["b"]
                    hdst, cdst = (hs_o, cs_o) if samp else (hp, cp)
                    prh = alloc_pair()
                    prc = alloc_pair()
                    for c in range(NCH):
                        P.op("pe", lambda e, c=c, b=b, prh=prh: e.transpose(out=ps[0:1, prh + c // 4, (c % 4) * 128:(c % 4 + 1) * 128],
                                                                           in_=hst[:, c, b:b + 1], identity=identf[:]),
                             reads=[("hst", c), ("identf",)], writes=fb(prh + c // 4))
                        c0 = (b * 35 + 32) if samp else ntok
                        P.op("pe", lambda e, c=c, c0=c0, prc=prc: e.transpose(out=ps[0:3, prc + c // 4, (c % 4) * 128:(c % 4 + 1) * 128],
                                                                             in_=xaT[:, c, c0:c0 + 3], identity=identf[:]),
                             reads=[("xam", c), ("identf",)], writes=fb(prc + c // 4))
                    sr = gv[b]
                    gvc_ = [("gv", b, q) for q in range(4)]
                    P.op("dve", lambda e, prh=prh, sr=sr: e.tensor_copy(out=sr[0:1, :].rearrange("p (a b) -> p a b", a=2), in_=ps[0:1, prh:prh + 2, :]),
                         reads=fb(prh) + fb(prh + 1), writes=gvc_)
                    P.op("sp", lambda e, bb=bb, hdst=hdst, sr=sr: e.dma_start(out=hdst[bb:bb + 1, :], in_=sr[0:1, :]),
                         reads=gvc_, dma_sem=so_sem, serial=True)
                    sr3 = tok[b]
                    P.op("dve", lambda e, prc=prc, sr3=sr3: e.tensor_copy(out=sr3[0:3, :].rearrange("p (a b) -> p a b", a=2), in_=ps[0:3, prc:prc + 2, :]),
                         reads=fb(prc) + fb(prc + 1), writes=[("tok", b)])
                    P.op("sp", lambda e, bb=bb, cdst=cdst, sr3=sr3: e.dma_start(out=cdst[bb, :, :], in_=sr3[0:3, :]),
                         reads=[("tok", b)], dma_sem=so_sem, serial=True)
                    release_pair(prh)
                    release_pair(prc)

            def mixer_stages(ti):
                T = tiles[ti]
                xb = ti % 3
                xnT = xnTs[ti % 2]
                ntok, nsub, PT = T["ntok"], T["nsub"], T["PT"]
                samp = T["kind"] == "s"
                xn_cells = [(("xnT", ti % 2), s) for s in range(nsub)]
                gcost = 8 * max(ntok, 64)
                st = []

                def segv(ap2d):
                    if samp:
                        return ap2d[:, 0:64].rearrange("p (b t) -> p b t", t=32)
                    return ap2d[:, 0:ntok]

                def xa_win(c, k):
                    if samp:
                        return xaT[:, c, 0:70].rearrange("p (b w) -> p b w", w=35)[:, :, k:k + 32]
                    return xaT[:, c, k:k + ntok]

                def s_init():
                    if T["first"]:
                        if samp:
                            for b in range(2):
                                cdma(lambda e, b=b: e.dma_start(out=hst[:, :, b], in_=sh_in[b].rearrange("(c p) -> p c", p=128),
                                                                allow_slow_non_contiguous=True),
                                     writes=[("hst", c) for c in range(NCH)])
                                for c in range(NCH):
                                    cdma(lambda e, b=b, c=c: e.dma_start(
                                        out=xaT[:, c, b * 35:b * 35 + 3],
                                        in_=sc_in[b, :, c * 128:(c + 1) * 128].rearrange("k p -> p k"),
                                        allow_slow_non_contiguous=True),
                                        writes=[("xah",)])
                        else:
                            P.op("pool", lambda e: e.memset(hst[:], 0.0), writes=[("hst", c) for c in range(NCH)])
                            P.op("pool", lambda e: e.memset(xaT[:, :, 0:3], 0.0), writes=[("xah",)])
                st.append((0, s_init))

                def s_loadx():
                    if ti + 1 < NT:
                        load_x(ti + 1)

                def s_xa(h):
                    slot, wv = ws_acquire(WIN[h])
                    b, pv = fm_pair(T, wv, xnT, xn_cells, slot)
                    if samp:
                        for m in range(2):
                            c = h * 2 + m
                            P.op("act", lambda e, c=c, m=m: e.activation(out=xa_win(c, 3), in_=segv(ps[:, b, m * 256:(m + 1) * 256]), func=AF.Copy),
                                 reads=fb(b), writes=[("xam", c)])
                    else:
                        P.op("act", lambda e: e.activation(out=xaT[:, 2 * h:2 * h + 2, 3:3 + ntok], in_=pv, func=AF.Copy),
                             reads=fb(b), writes=[("xam", 2 * h), ("xam", 2 * h + 1)])
                    ws_release()

                def s_conv(c):
                    P.op("dve", lambda e: e.tensor_scalar(out=segv(xcf[:, c, :]), in0=xa_win(c, 0), scalar1=cw[:, c, 0:1],
                                                          scalar2=cbT[:, c:c + 1], op0=ALU.mult, op1=ALU.add),
                         reads=[("xam", c), ("xah",), ("cw",), ("cbT",)], writes=[("xcf", c)])
                    for k in range(1, 4):
                        P.op("dve", lambda e, k=k: e.scalar_tensor_tensor(out=segv(xcf[:, c, :]), in0=xa_win(c, k),
                                                                          scalar=cw[:, c, k:k + 1], in1=segv(xcf[:, c, :]),
                                                                          op0=ALU.mult, op1=ALU.add),
                             reads=[("xam", c), ("xah",), ("cw",), ("xcf", c)], writes=[("xcf", c)])
                    if c >= 1:
                        s_xcb1(c - 1)
                    if c == NCH - 1 and not samp and not T["last"]:
                        P.op("dve", lambda e: e.tensor_copy(out=xaT[:, :, 0:3], in_=xaT[:, :, ntok:ntok + 3]),
                             reads=[("xam", cc) for cc in range(NCH)], writes=[("xah",)])

                def s_xcb1(c):
                    P.op("act", lambda e: e.activation(out=xcb[:, c, 0:ntok], in_=xcf[:, c, 0:ntok], func=AF.Copy),
                         reads=[("xcf", c)], writes=[("xcb", c)])

                def s_xcb():
                    s_xcb1(NCH - 1)

                def s_ga(h):
                    slot, wv = ws_acquire(WIN[4 + h])
                    b, pv = fm_pair(T, wv, xnT, xn_cells, slot)
                    P.op("act", lambda e: e.activation(out=gga[:, 2 * h:2 * h + 2, 0:ntok], in_=pv, func=AF.Gelu_apprx_tanh),
                         reads=fb(b), writes=[("gga", 2 * h), ("gga", 2 * h + 1)])
                    ws_release()

                def s_gate(c):
                    bg_ = alloc_bank()
                    P.op("pe", lambda e: e.matmul(ps[:, bg_, 0:ntok], lhsT=wab[:, c, :], rhs=xcb[:, c, 0:ntok], start=True, stop=True),
                         reads=[("wab",), ("xcb", c)], writes=fb(bg_))
                    P.op("pe", lambda e: e.matmul(ps[:, bg_, 256:256 + ntok], lhsT=wxb[:, c, :], rhs=xcb[:, c, 0:ntok], start=True, stop=True),
                         reads=[("wxb",), ("xcb", c)], writes=fb(bg_))
                    tr = rotn("chTr", 2)
                    sl = c % 4
                    Tr, Ti, Aa, Ee = chTr[tr], chTi[sl], chA[sl], chE[sl]
                    P.op("act", lambda e: e.activation(out=Tr[:, 0:ntok], in_=ps[:, bg_, 0:ntok], func=AF.Tanh,
                                                       scale=0.5, bias=ba2[:, c:c + 1]),
                         reads=fb(bg_) + [("ba2",)], writes=[("chTr", tr)])
                    P.op("act", lambda e: e.activation(out=Ti[:, 0:ntok], in_=ps[:, bg_, 256:256 + ntok], func=AF.Tanh,
                                                       scale=0.5, bias=bx2[:, c:c + 1]),
                         reads=fb(bg_) + [("bx2",)], writes=[("chTi", sl)])
                    P.op("act", lambda e: e.activation(out=Aa[:, 0:ntok], in_=Tr[:, 0:ntok], func=AF.Exp,
                                                       scale=hcn[:, c:c + 1], bias=hcn[:, c:c + 1]),
                         reads=[("chTr", tr), ("hcn",)], writes=[("chA", sl)])
                    P.op("act", lambda e: e.activation(out=Ee[:, 0:ntok], in_=Tr[:, 0:ntok], func=AF.Exp,
                                                       scale=cneg[:, c:c + 1], bias=cneg[:, c:c + 1]),
                         reads=[("chTr", tr), ("cneg",)], writes=[("chE", sl)])
                    P.op("dve", lambda e: e.scalar_tensor_tensor(out=Ti[:, 0:ntok], in0=Ti[:, 0:ntok], scalar=1.0,
                                                                 in1=xcf[:, c, 0:ntok], op0=ALU.add, op1=ALU.mult),
                         reads=[("chTi", sl), ("xcf", c)], writes=[("chTi", sl)])

                def s_sqrt(grp):
                    for c in range(grp * 4, grp * 4 + 4):
                        sl = c % 4
                        Ee = chE[sl]
                        P.op("act", lambda e, Ee=Ee: e.activation(out=Ee[:, 0:ntok], in_=Ee[:, 0:ntok], func=AF.Sqrt,
                                                                 scale=-0.25, bias=0.25),
                             reads=[("chE", sl)], writes=[("chE", sl)])

                def s_scan(c):
                    sl = c % 4
                    Ti, Aa, Ee = chTi[sl], chA[sl], chE[sl]
                    hh = rotn("chH", 2)
                    Hh = chH[hh]
                    P.op("dve", lambda e: e.tensor_tensor(out=Ti[:, 0:ntok], in0=Ee[:, 0:ntok], in1=Ti[:, 0:ntok], op=ALU.mult),
                         reads=[("chTi", sl), ("chE", sl)], writes=[("chTi", sl)])
                    nseg = 2 if samp else 1
                    sl_len = 32 if samp else ntok
                    for b in range(nseg):
                        c0 = b * sl_len
                        P.op("dve", lambda e, b=b, c0=c0: e.tensor_tensor_scan(
                            out=Hh[:, c0:c0 + sl_len], data0=Aa[:, c0:c0 + sl_len], data1=Ti[:, c0:c0 + sl_len],
                            initial=hst[:, c, b:b + 1], op0=ALU.mult, op1=ALU.add),
                            reads=[("chA", sl), ("chTi", sl), ("hst", c), ("chH", hh)], writes=[("chH", hh)])
                    if samp:
                        P.op("dve", lambda e: e.tensor_copy(
                            out=hst[:, c, 0:2], in_=Hh[:, 0:64].rearrange("p (b t) -> p b t", t=32)[:, :, 31]),
                            reads=[("chH", hh)], writes=[("hst", c)])
                    else:
                        P.op("dve", lambda e: e.tensor_copy(out=hst[:, c, 0:1], in_=Hh[:, ntok - 1:ntok]),
                             reads=[("chH", hh)], writes=[("hst", c)])
                    P.op("dve", lambda e: e.tensor_tensor(out=hgT[:, c, 0:ntok], in0=Hh[:, 0:ntok], in1=gga[:, c, 0:ntok], op=ALU.mult),
                         reads=[("chH", hh), ("gga", c)], writes=[("hgT", c)])

                def s_u(h):
                    slot, wv = ws_acquire(WIN[8 + h])
                    b, pv = fm_pair(T, wv, xnT, xn_cells, slot)
                    P.op("act", lambda e: e.activation(out=gu[:, 2 * h:2 * h + 2, 0:ntok], in_=pv, func=AF.Gelu_apprx_tanh),
                         reads=fb(b), writes=[("gu", 2 * h), ("gu", 2 * h + 1)])
                    ws_release()

                def s_v(h):
                    slot, wv = ws_acquire(WIN[12 + h])
                    for s in range(nsub):
                        gsl = s % 2
                        bk = alloc_half()
                        for k in range(NCH):
                            P.op("pe", lambda e, k=k, bk=bk, s=s: e.matmul(psh[0:PT, bk, 0:256], lhsT=xnT[:, k, s * 128:s * 128 + PT],
                                                                          rhs=wv[:, k, :], start=(k == 0), stop=(k == NCH - 1)),
                                 reads=[("wr", slot), (("xnT", ti % 2), s)], writes=[("ps", bk)])
                        P.op("act", lambda e, bk=bk, gsl=gsl: e.activation(out=gv[gsl][0:PT, h * 256:(h + 1) * 256],
                                                                           in_=psh[0:PT, bk, 0:256], func=AF.Gelu_apprx_tanh),
                             reads=[("ps", bk)], writes=[("gv", gsl, h)])
                        P.op("dve", lambda e, gsl=gsl: e.bn_stats(out=st6[gsl][0:PT, h, :], in_=gv[gsl][0:PT, h * 256:(h + 1) * 256]),
                             reads=[("gv", gsl, h)], writes=[("st6", gsl, h)])
                    ws_release()

                def s_ln():
                    rr = []
                    for s in range(nsub):
                        gsl = s % 2
                        P.op("dve", lambda e, gsl=gsl: e.bn_aggr(out=mvt[gsl][0:PT, :], in_=st6[gsl][0:PT].rearrange("p a b -> p (a b)")),
                             reads=[("st6", gsl, q) for q in range(4)], writes=[("mvt", gsl)])
                        vsc = stat()
                        P.op("pool", lambda e, gsl=gsl, vsc=vsc: e.tensor_copy(out=stats[0:PT, vsc:vsc + 1], in_=mvt[gsl][0:PT, 1:2]),
                             reads=[("mvt", gsl)], writes=[("st", vsc)])
                        rr.append(rstd_from(vsc, PT, 1.0, EPS))
                    for s in range(nsub):
                        gsl = s % 2
                        r = rr[s]
                        gvc = [("gv", gsl, q) for q in range(4)]
                        P.op("dve", lambda e, gsl=gsl, r=r: e.tensor_scalar(out=gv[gsl][0:PT, :], in0=gv[gsl][0:PT, :],
                                                                           scalar1=mvt[gsl][0:PT, 0:1], scalar2=stats[0:PT, r:r + 1],
                                                                           op0=ALU.subtract, op1=ALU.mult),
                             reads=gvc + [("mvt", gsl), ("st", r)], writes=gvc)
                        P.op("pool", lambda e, gsl=gsl: e.tensor_tensor(out=gv[gsl][0:PT, :], in0=gv[gsl][0:PT, :], in1=gbc_lng[0:PT, :],
                                                                       op=ALU.mult),
                             reads=gvc + [("gbc_lng",)], writes=gvc)

                def s_ln2():
                    for s in range(nsub):
                        gsl = s % 2
                        gvc = [("gv", gsl, q) for q in range(4)]
                        if samp:
                            P.op("dve", lambda e, gsl=gsl: e.tensor_tensor(out=gv[gsl][0:PT, :], in0=gv[gsl][0:PT, :], in1=gbc_lnb[0:PT, :],
                                                                          op=ALU.add),
                                 reads=gvc + [("gbc_lnb",)], writes=gvc)
                            P.op("act", lambda e, gsl=gsl: e.activation(out=vn[gsl][0:PT, :], in_=gv[gsl][0:PT, :], func=AF.Copy),
                                 reads=gvc, writes=[("vn", gsl)])
                            P.op("sp", lambda e, gsl=gsl: e.dma_start(out=vs_o.rearrange("b t d -> (b t) d"), in_=gv[gsl][0:PT, :]),
                                 reads=gvc, dma_sem=vs_sem)
                        else:
                            P.op("dve", lambda e, gsl=gsl: e.tensor_tensor(out=vn[gsl][0:PT, :], in0=gv[gsl][0:PT, :], in1=gbc_lnb[0:PT, :],
                                                                          op=ALU.add),
                                 reads=gvc + [("gbc_lnb",)], writes=[("vn", gsl)])

                prep_rs = []

                def s_states():
                    if T["last"]:
                        emit_states(T)

                def s_prepA():
                    if ti + 1 < NT:
                        prep_rs.extend(norm_A(tiles[ti + 1], (ti + 1) % 3))

                def s_prepB():
                    if ti + 1 < NT:
                        norm_B(tiles[ti + 1], (ti + 1) % 3, prep_rs)

                def s_prepC():
                    if ti + 1 < NT:
                        norm_C(tiles[ti + 1], gT_pm, "gT_pm", xnTs[(ti + 1) % 2], ("xnT", (ti + 1) % 2))

                def s_spatial(gp):
                    b = alloc_bank()
                    for gi in range(2):
                        g = 2 * gp + gi
                        if samp:
                            for bb in range(2):
                                wsb_ = wsS0 if bb == 0 else wsS1
                                P.op("pe", lambda e, bb=bb, wsb_=wsb_, g=g, gi=gi: e.matmul(
                                    ps[:, b, gi * 256 + bb * 32:gi * 256 + (bb + 1) * 32], lhsT=vn[0][0:64, g * 128:(g + 1) * 128],
                                    rhs=wsb_[:, g, :], start=True, stop=False),
                                    reads=[("vn", 0), ("wsS0",), ("wsS1",)], writes=fb(b))
                                P.op("pe", lambda e, bb=bb, g=g, gi=gi: e.matmul(
                                    ps[:, b, gi * 256 + bb * 32:gi * 256 + (bb + 1) * 32], lhsT=ones128[0:64, :],
                                    rhs=bs128[0:64, g, 0:32], start=False, stop=True),
                                    reads=[("ones128",), ("bs128",)], writes=fb(b))
                        else:
                            for s in range(nsub):
                                P.op("pe", lambda e, s=s, g=g, gi=gi: e.matmul(
                                    ps[:, b, gi * 256 + s * 128:gi * 256 + (s + 1) * 128], lhsT=vn[s % 2][:, g * 128:(g + 1) * 128],
                                    rhs=wsT[:, g, :], start=True, stop=False),
                                    reads=[("vn", s % 2), ("wsT",)], writes=fb(b))
                                P.op("pe", lambda e, s=s, g=g, gi=gi: e.matmul(
                                    ps[:, b, gi * 256 + s * 128:gi * 256 + (s + 1) * 128], lhsT=ones128[:, :],
                                    rhs=bs128[:, g, :], start=False, stop=True),
                                    reads=[("ones128",), ("bs128",)], writes=fb(b))
                    pv = ps[:, b, :].rearrange("p (m c) -> p m c", m=2)[:, :, 0:ntok]
                    P.op("dve", lambda e: e.tensor_tensor(out=gsT[:, 2 * gp:2 * gp + 2, 0:ntok], in0=gu[:, 2 * gp:2 * gp + 2, 0:ntok], in1=pv,
                                                          op=ALU.mult),
                         reads=[("gu", 2 * gp), ("gu", 2 * gp + 1)] + fb(b), writes=[("gsT", 2 * gp), ("gsT", 2 * gp + 1)])

                gs_cells = [("gsT", g) for g in range(8)]
                hg_cells = [("hgT", c) for c in range(NCH)]

                def s_gb(mh):
                    slot, wv = ws_acquire(WIN[20 + mh])
                    b, pv = fm_pair(T, wv, xnT, xn_cells, slot)
                    P.op("act", lambda e: e.activation(out=Tsig[:, :, 0:ntok], in_=pv, func=AF.Tanh, scale=0.5),
                         reads=fb(b), writes=[("Tsig", 0), ("Tsig", 1)])
                    ws_release()

                def s_ob(mh):
                    slot, wv = ws_acquire(BRB[mh])
                    b, pv = fm_pair(T, wv, gsT, gs_cells, slot)
                    P.op("dve", lambda e: e.scalar_tensor_tensor(out=t2[:, 2 * mh:2 * mh + 2, 0:ntok], in0=Tsig[:, :, 0:ntok], scalar=1.0,
                                                                 in1=pv, op0=ALU.add, op1=ALU.mult),
                         reads=[("Tsig", 0), ("Tsig", 1)] + fb(b), writes=[("t2", 2 * mh), ("t2", 2 * mh + 1)])
                    ws_release()

                def s_ga2(mh):
                    slot, wv = ws_acquire(WIN[16 + mh])
                    b, pv = fm_pair(T, wv, xnT, xn_cells, slot)
                    P.op("act", lambda e: e.activation(out=Tsig[:, :, 0:ntok], in_=pv, func=AF.Tanh, scale=0.5),
                         reads=fb(b), writes=[("Tsig", 0), ("Tsig", 1)])
                    ws_release()

                def s_oa(mh):
                    slot, wv = ws_acquire(BRA[mh])
                    b, pv = fm_pair(T, wv, hgT, hg_cells, slot)
                    P.op("dve", lambda e: e.scalar_tensor_tensor(out=tmpA2[:, :, 0:ntok], in0=Tsig[:, :, 0:ntok], scalar=1.0,
                                                                 in1=pv, op0=ALU.add, op1=ALU.mult),
                         reads=[("Tsig", 0), ("Tsig", 1)] + fb(b), writes=[("tmpA2",)])
                    P.op("pool", lambda e: e.tensor_tensor(out=mixT[:, 2 * mh:2 * mh + 2, 0:ntok], in0=tmpA2[:, :, 0:ntok],
                                                           in1=t2[:, 2 * mh:2 * mh + 2, 0:ntok], op=ALU.add),
                         reads=[("tmpA2",), ("t2", 2 * mh), ("t2", 2 * mh + 1)], writes=[("mixT", 2 * mh), ("mixT", 2 * mh + 1)])
                    ws_release()

                mix_cells = [("mixT", c) for c in range(NCH)]
                prs = []

                def s_out(q):
                    if q == 0:
                        for s in range(nsub):
                            prs.append(alloc_pair())
                    slot, wv = ws_acquire(WOUT[q])
                    for s in range(nsub):
                        bkq = prs[s] + q // 2
                        cq = (q % 2) * 256
                        for k in range(NCH):
                            P.op("pe", lambda e, k=k, s=s, bkq=bkq, cq=cq: e.matmul(
                                ps[0:PT, bkq, cq:cq + 256], lhsT=mixT[:, k, s * 128:s * 128 + PT], rhs=wv[:, k, :],
                                start=(k == 0), stop=(k == NCH - 1)),
                                reads=[("wr", slot)] + mix_cells, writes=[("ps", 2 * bkq + (q % 2))])
                    ws_release()

                pn_rs = []
                hn_rs = []

                def s_pnA():
                    for s in range(nsub):
                        pn_rs.append(post_norm_A(T, prs[s], eps=4.0 * EPS))

                def s_pnB():
                    for s in range(nsub):
                        post_norm_B(T, xb, s, prs[s], pn_rs[s], gbc_pm, "gbc_pm")
                    hn_rs.extend(norm_A(T, xb))

                def s_hnB():
                    norm_B(T, xb, hn_rs)

                US = 3400
                for h in range(4):
                    st.append((2 * gcost, lambda h=h: s_xa(h)))
                for h in range(4):
                    st.append((int(1.8 * US), lambda c=2 * h: s_conv(c)))
                    st.append((nsub * 8 * 256, lambda h=h: s_v(h)))
                    st.append((int(1.8 * US), lambda c=2 * h + 1: s_conv(c)))
                st.append((0, s_xcb))
                st.append((int(2.5 * US), s_ln))
                for h in range(4):
                    st.append((2 * gcost, lambda h=h: s_ga(h)))
                    if h == 1:
                        st.append((int(2.5 * US), s_ln2))
                for c in range(4):
                    st.append((int(1.8 * US), lambda c=c: s_gate(c)))
                    st.append((2 * gcost, lambda h=c: s_u(h)))
                st.append((int(2.5 * US), lambda: s_sqrt(0)))
                for g in range(2):
                    st.append((int(1.8 * US), lambda c=2 * g: s_scan(c)))
                    st.append((8 * ntok, lambda g=g: s_spatial(g)))
                    st.append((int(1.8 * US), lambda c=2 * g + 1: s_scan(c)))
                for c in range(4, 8):
                    st.append((int(1.8 * US), lambda c=c: s_gate(c)))
                    if c % 2 == 1:
                        st.append((8 * ntok, lambda g=(c - 1) // 2: s_spatial(g)))
                st.append((int(2.5 * US), lambda: s_sqrt(1)))
                st.append((0, s_loadx))
                st.append((int(2.0 * US), s_prepA))
                for mh in range(4):
                    st.append((2 * gcost, lambda mh=mh: s_gb(mh)))
                    st.append((int(1.8 * US), lambda c=4 + mh: s_scan(c)))
                    st.append((2 * gcost, lambda mh=mh: s_ob(mh)))
                    if mh == 0:
                        st.append((int(2.0 * US), s_prepB))
                    if mh == 2:
                        st.append((int(3.0 * US), s_prepC))
                st.append((0, s_states))
                for mh in range(4):
                    st.append((2 * gcost, lambda mh=mh: s_ga2(mh)))
                    st.append((2 * gcost, lambda mh=mh: s_oa(mh)))
                for q in range(4):
                    st.append((nsub * 8 * 256, lambda q=q: s_out(q)))
                st.append((int(2.0 * US), s_pnA))
                st.append((int(3.5 * US), s_pnB))
                st.append((int(2.0 * US), s_hnB))
                return st

            def ffn_stages(ti):
                T = tiles[ti]
                xb = ti % 3
                ntok, nsub, PT = T["ntok"], T["nsub"], T["PT"]
                samp = T["kind"] == "s"
                hn_cells = [("hnT", s) for s in range(nsub)]
                ff_cells = [("ffT", j) for j in range(NFF)]
                gcost = 8 * max(ntok, 64)
                st = []
                prs = []

                def s_fin(q):
                    slg, wvg = ws_acquire(FING[q])
                    slu, wvu = ws_acquire(FINU[q])
                    for jj in range(2):
                        j = 2 * q + jj
                        b = alloc_bank()
                        for m, (wv_, sl_) in enumerate(((wvg, slg), (wvu, slu))):
                            for k in range(NCH):
                                P.op("pe", lambda e, k=k, m=m, wv_=wv_, jj=jj, b=b: e.matmul(
                                    ps[:, b, m * 256:m * 256 + ntok], lhsT=wv_[:, k, jj * 128:(jj + 1) * 128],
                                    rhs=hnT[:, k, 0:ntok], start=(k == 0), stop=(k == NCH - 1)),
                                    reads=[("wr", sl_)] + hn_cells, writes=fb(b))
                        pv = ps[:, b, :].rearrange("p (m c) -> p m c", m=2)[:, :, 0:ntok]
                        sg = rotn("sgt", 2)
                        P.op("act", lambda e, pv=pv, sg=sg: e.activation(out=sgt[sg][:, 0:ntok], in_=pv[:, 0, :], func=AF.Tanh, scale=0.5),
                             reads=fb(b), writes=[("sgt", sg)])
                        P.op("dve", lambda e, pv=pv, sg=sg: e.scalar_tensor_tensor(out=sgt[sg][:, 0:ntok], in0=sgt[sg][:, 0:ntok], scalar=1.0,
                                                                               in1=pv[:, 0, :], op0=ALU.add, op1=ALU.mult),
                             reads=[("sgt", sg)] + fb(b), writes=[("sgt", sg)])
                        P.op("dve", lambda e, pv=pv, sg=sg, j=j: e.scalar_tensor_tensor(out=ffT[:, j, 0:ntok], in0=sgt[sg][:, 0:ntok], scalar=0.5,
                                                                                    in1=pv[:, 1, :], op0=ALU.mult, op1=ALU.mult),
                             reads=[("sgt", sg)] + fb(b), writes=[("ffT", j)])
                    ws_release()
                    ws_release()

                def s_fout(r):
                    if r == 0:
                        for s in range(nsub):
                            prs.append(alloc_pair())
                    slot, wv = ws_acquire(FOUT[r])
                    for s in range(nsub):
                        for half in range(2):
                            for kk in range(2):
                                j = r * 2 + kk
                                P.op("pe", lambda e, j=j, kk=kk, s=s, half=half: e.matmul(
                                    ps[0:PT, prs[s] + half, :], lhsT=ffT[:, j, s * 128:s * 128 + PT],
                                    rhs=wv[:, kk, half * 512:(half + 1) * 512], start=(j == 0), stop=(j == NFF - 1)),
                                    reads=[("wr", slot)] + ff_cells, writes=fb(prs[s] + half))
                    ws_release()

                fin_rs = []

                def s_hnC():
                    norm_C(T, gT_pf, "gT_pf", hnT, "hnT")

                def s_finalA():
                    for s in range(nsub):
                        fin_rs.append(post_norm_A(T, prs[s]))

                def s_final():
                    for s in range(nsub):
                        post_norm_B(T, xb, s, prs[s], fin_rs[s], gbc_pf, "gbc_pf")
                        if samp:
                            P.op("sp", lambda e: e.dma_start(out=ys.rearrange("b t d -> (b t) d"), in_=xbuf[xb][0:PT, 0, :]),
                                 reads=[("x", xb, 0)], dma_sem=y_sems[xb])
                        else:
                            r0 = T["tt"] * TT + s * 128
                            P.op("sp", lambda e, s=s, r0=r0: e.dma_start(out=yp[T["b"], r0:r0 + 128, :], in_=xbuf[xb][:, s, :]),
                                 reads=[("x", xb, s)], dma_sem=y_sems[xb])

                st.append((int(3.0 * 2400), s_hnC))
                for q in range(NFF // 2):
                    st.append((4 * gcost, lambda q=q: s_fin(q)))
                for r in range(NFF // 2):
                    st.append((int(FOUT_W * nsub * 4 * 512), lambda r=r: s_fout(r)))
                st.append((int(2.0 * 2400), s_finalA))
                st.append((int(3.5 * 2400), s_final))
                return st

            def timed(stages, t0):
                tot = sum(c for c, _ in stages) + 1e-9
                out = []
                acc = 0.0
                for c, f in stages:
                    out.append((t0 + acc / tot, f))
                    acc += c
                return out

            ws_prefetch()
            load_x(0)
            prep(0)
            allst = []
            for ti in range(NT):
                allst += [(t, 0, i, f) for i, (t, f) in enumerate(timed(mixer_stages(ti), float(ti)))]
                allst += [(t, 1, i, f) for i, (t, f) in enumerate(timed(ffn_stages(ti), ti + 1.0 + F_OFFSET))]
            if not INTERLEAVE:
                allst = [(float(int(t - (1.0 + F_OFFSET if k else 0.0)) + 0.5 * k), k, i, f) for (t, k, i, f) in allst]
            allst.sort(key=lambda z: (z[0], z[1], z[2]))
            for (_, _, _, f) in allst:
                f()

        record(_Dry(), True)
        P = Prog()
        record(P, False)
        P.emit(block, eng_sems, {"sp": [y_sems[0], y_sems[1], y_sems[2], so_sem, vs_sem]})
    return nc


_WNAMES = ["g_pre_mix", "w_in", "conv_w", "conv_b", "w_a", "b_a", "w_x", "b_x", "lam", "w_br_a", "ln_g", "ln_b",
           "w_s", "b_s", "w_br_b", "w_out", "g_post_mix", "g_pre_ffn", "w_ffn_in", "w_ffn_out", "g_post_ffn"]


def make_in_maps(inputs):
    f = lambda a: np.ascontiguousarray(np.asarray(a, dtype=np.float32))
    shared = {}
    for n in _WNAMES:
        a = f(inputs[n])[0]
        if n in ("b_a", "b_x"):
            a = a.reshape(-1)
        shared[n] = np.ascontiguousarray(a)
    xpr, xsm = f(inputs["x_prompt"]), f(inputs["x_sample"])
    sh, sc = f(inputs["state_rglru_h"])[0], f(inputs["state_rglru_conv"])[0]
    maps = []
    for i in range(NCORES):
        m = dict(shared)
        m["xp"] = np.ascontiguousarray(xpr[2 * i:2 * i + 2])
        m["xs"] = np.ascontiguousarray(xsm[2 * i:2 * i + 2])
        m["sh"] = np.ascontiguousarray(sh[2 * i:2 * i + 2])
        m["sc"] = np.ascontiguousarray(sc[2 * i:2 * i + 2])
        maps.append(m)
    return maps


def kernel(**inputs):
    nc = build_program()
    in_maps = make_in_maps(inputs)
    res = run_bass_kernel_spmd(nc, in_maps, core_ids=list(range(NCORES)))
    R = res.results
    cat = lambda k: np.concatenate([np.asarray(r[k], dtype=np.float32) for r in R], axis=0)
    y_prompt = cat("yp")
    y_sample = cat("ys")
    new_h_prompt = cat("hp")[None]
    new_conv_prompt = cat("cp")[None]
    new_h_sample = cat("hs")[None]
    new_conv_sample = cat("cs")[None]
    new_v_sample = cat("vs")[None]
    return (y_prompt, y_sample, new_h_prompt, new_conv_prompt, new_h_sample, new_conv_sample, new_v_sample)
```
